# Optimizing a Trainium2 kernel written in Bass

```python
import jax, jax.numpy as jnp
from jax import lax
import numpy as np

D_MODEL = 1024
BATCH = 8
SEQ = 4096
DEPTH = 1

CHUNK = 64
MEM_LEN = 256
GLA_HEADS = 4
GLA_DK = D_MODEL // 2
GLA_DV = D_MODEL
GLA_HDK = GLA_DK // GLA_HEADS
GLA_HDV = GLA_DV // GLA_HEADS
GLA_GATE_RANK = 16
GLA_GATE_TEMP = 16.0
POOL_WINDOWS = (2, 4, 8, 16)
POOL_GROUPS = len(POOL_WINDOWS)
POOL_WIDTH = D_MODEL // 2
POOL_GROUP_DIM = POOL_WIDTH // POOL_GROUPS
XA_HEADS = 4
XA_HEAD_DIM = 128
XA_WIDTH = XA_HEADS * XA_HEAD_DIM
N_BRANCH = 3
D_FF = 2816
EPS = 1e-6

IN_SPLITS = (GLA_DK, GLA_DK, GLA_DV, GLA_DV, GLA_GATE_RANK, POOL_WIDTH, XA_WIDTH, N_BRANCH * D_MODEL)
IN_WIDTH = sum(IN_SPLITS)

kernel_name = "hybrid_gla_pool_memxattn_macaron_block"


def rms_norm(x, g):
    xf = x.astype(jnp.float32)
    y = xf * lax.rsqrt(jnp.mean(xf * xf, axis=-1, keepdims=True) + EPS)
    return (y * g.astype(jnp.float32)).astype(x.dtype)


def swiglu(h, w_in, w_out):
    a, b = jnp.split(h @ w_in, 2, axis=-1)
    return (jax.nn.silu(a) * b) @ w_out


def gla_chunked(q, k, v, log_a):
    B, S = q.shape[0], q.shape[1]
    nc = S // CHUNK

    def to_chunks(t):
        return t.reshape(B, nc, CHUNK, GLA_HEADS, t.shape[-1]).transpose(1, 0, 3, 2, 4)

    qc, kc, vc = to_chunks(q), to_chunks(k), to_chunks(v)
    b = jnp.cumsum(to_chunks(log_a.astype(jnp.float32)), axis=3)
    b_end = b[:, :, :, -1:, :]
    kt = (kc.astype(jnp.float32) * jnp.exp(b_end - b)).astype(v.dtype)
    decay = jnp.exp(b_end[:, :, :, 0, :]).astype(v.dtype)

    def step(state, inp):
        q_c, k_c, v_c, d_c = inp
        state = d_c[..., None] * state + jnp.einsum('bhck,bhcv->bhkv', k_c, v_c)
        o = jnp.einsum('bhck,bhkv->bhcv', q_c, state)
        return state, o

    s0 = jnp.zeros((B, GLA_HEADS, GLA_HDK, GLA_HDV), v.dtype)
    _, o = lax.scan(step, s0, (qc, kt, vc, decay))
    return o.transpose(1, 0, 3, 2, 4).reshape(B, S, GLA_DV)


def multiscale_pool(p, w_pool, pool_scale):
    B, S, _ = p.shape
    pg = p.reshape(B, S, POOL_GROUPS, POOL_GROUP_DIM).astype(jnp.float32)
    c0 = jnp.concatenate([jnp.zeros((B, 1, POOL_GROUPS, POOL_GROUP_DIM), jnp.float32),
                          jnp.cumsum(pg, axis=1)], axis=1)
    pos = jnp.arange(1, S + 1, dtype=jnp.float32)
    outs = []
    for g, w in enumerate(POOL_WINDOWS):
        cg = c0[:, :, g]
        lag = jnp.concatenate([jnp.zeros((B, w - 1, POOL_GROUP_DIM), jnp.float32), cg[:, :S + 1 - w]], axis=1)
        cnt = jnp.minimum(pos, float(w))[None, :, None]
        outs.append((cg[:, 1:] - lag) / cnt - pg[:, :, g])
    mixed = jnp.stack(outs, axis=2).astype(p.dtype)
    y = jnp.einsum('bsgc,gcd->bsgd', mixed, w_pool).reshape(B, S, POOL_WIDTH)
    return y * pool_scale


def memory_cross_attention(xq, mem_n, w_mem_kv):
    B, S, _ = xq.shape
    M = mem_n.shape[1]
    q = xq.reshape(B, S, XA_HEADS, XA_HEAD_DIM)
    k, v = jnp.split(mem_n @ w_mem_kv, 2, axis=-1)
    k = k.reshape(B, M, XA_HEADS, XA_HEAD_DIM)
    v = v.reshape(B, M, XA_HEADS, XA_HEAD_DIM)
    s = jnp.einsum('bshd,bmhd->bhsm', q, k).astype(jnp.float32) * (XA_HEAD_DIM ** -0.5)
    pr = jax.nn.softmax(s, axis=-1).astype(v.dtype)
    return jnp.einsum('bhsm,bmhd->bshd', pr, v).reshape(B, S, XA_WIDTH)


def token_mixing(h, mem, w_in, w_fu, b_f, gla_norm_g, w_pool, pool_scale, mem_norm_g, w_mem_kv,
                 w_up_gla, w_up_pool, w_up_xattn, w_o):
    B, S, _ = h.shape
    idx = np.cumsum(IN_SPLITS)[:-1].tolist()
    q, k, v, g_out, f_low, p_in, xq, gates = jnp.split(h @ w_in, idx, axis=-1)
    q = q.reshape(B, S, GLA_HEADS, GLA_HDK) * (GLA_HDK ** -0.5)
    k = k.reshape(B, S, GLA_HEADS, GLA_HDK)
    v = v.reshape(B, S, GLA_HEADS, GLA_HDV)
    f = (f_low @ w_fu + b_f).astype(jnp.float32)
    log_a = (jax.nn.log_sigmoid(f) / GLA_GATE_TEMP).reshape(B, S, GLA_HEADS, GLA_HDK)
    o = gla_chunked(q, k, v, log_a).reshape(B, S, GLA_HEADS, GLA_HDV)
    o = rms_norm(o, gla_norm_g.reshape(GLA_HEADS, GLA_HDV)).reshape(B, S, GLA_DV)
    y_a = (o * jax.nn.silu(g_out)) @ w_up_gla
    y_b = multiscale_pool(p_in, w_pool, pool_scale) @ w_up_pool
    y_c = memory_cross_attention(xq, rms_norm(mem, mem_norm_g), w_mem_kv) @ w_up_xattn
    gt = jax.nn.sigmoid(gates.reshape(B, S, N_BRANCH, D_MODEL))
    merged = gt[:, :, 0] * y_a + gt[:, :, 1] * y_b + gt[:, :, 2] * y_c
    return merged @ w_o


def setup_inputs(seed: int = 0) -> dict:
    key = jax.random.key(seed)
    ks = jax.random.split(key, 32)
    L = DEPTH

    def dense(k, shape, fan_in):
        return jax.random.normal(k, shape, jnp.float32) * (fan_in ** -0.5)

    def gain(k, n):
        return 1.0 + 0.02 * jax.random.normal(k, (L, n), jnp.float32)

    return {
        "x": jax.random.normal(ks[0], (BATCH, SEQ, D_MODEL), jnp.float32),
        "mem": jax.random.normal(ks[1], (BATCH, MEM_LEN, D_MODEL), jnp.float32),
        "ffn1_pre_g": gain(ks[2], D_MODEL),
        "ffn1_w_in": dense(ks[3], (L, D_MODEL, 2 * D_FF), D_MODEL),
        "ffn1_w_out": dense(ks[4], (L, D_FF, D_MODEL), D_FF),
        "ffn1_post_g": gain(ks[5], D_MODEL),
        "mix_pre_g": gain(ks[6], D_MODEL),
        "w_in": dense(ks[7], (L, D_MODEL, IN_WIDTH), D_MODEL),
        "w_fu": dense(ks[8], (L, GLA_GATE_RANK, GLA_DK), GLA_GATE_RANK),
        "b_f": 0.1 * jax.random.normal(ks[9], (L, GLA_DK), jnp.float32),
        "gla_norm_g": gain(ks[10], GLA_DV),
        "w_pool": dense(ks[11], (L, POOL_GROUPS, POOL_GROUP_DIM, POOL_GROUP_DIM), POOL_GROUP_DIM),
        "pool_scale": gain(ks[12], POOL_WIDTH),
        "mem_norm_g": gain(ks[13], D_MODEL),
        "w_mem_kv": dense(ks[14], (L, D_MODEL, 2 * XA_WIDTH), D_MODEL),
        "w_up_gla": dense(ks[15], (L, GLA_DV, D_MODEL), GLA_DV),
        "w_up_pool": dense(ks[16], (L, POOL_WIDTH, D_MODEL), POOL_WIDTH),
        "w_up_xattn": dense(ks[17], (L, XA_WIDTH, D_MODEL), XA_WIDTH),
        "w_o": dense(ks[18], (L, D_MODEL, D_MODEL), D_MODEL),
        "mix_post_g": gain(ks[19], D_MODEL),
        "ffn2_pre_g": gain(ks[20], D_MODEL),
        "ffn2_w_in": dense(ks[21], (L, D_MODEL, 2 * D_FF), D_MODEL),
        "ffn2_w_out": dense(ks[22], (L, D_FF, D_MODEL), D_FF),
        "ffn2_post_g": gain(ks[23], D_MODEL),
        "final_g": gain(ks[24], D_MODEL),
    }


def reference(x, mem, ffn1_pre_g, ffn1_w_in, ffn1_w_out, ffn1_post_g, mix_pre_g, w_in, w_fu, b_f,
              gla_norm_g, w_pool, pool_scale, mem_norm_g, w_mem_kv, w_up_gla, w_up_pool, w_up_xattn,
              w_o, mix_post_g, ffn2_pre_g, ffn2_w_in, ffn2_w_out, ffn2_post_g, final_g):
    for l in range(DEPTH):
        x = x + 0.5 * rms_norm(swiglu(rms_norm(x, ffn1_pre_g[l]), ffn1_w_in[l], ffn1_w_out[l]), ffn1_post_g[l])
        h = rms_norm(x, mix_pre_g[l])
        y = token_mixing(h, mem, w_in[l], w_fu[l], b_f[l], gla_norm_g[l], w_pool[l], pool_scale[l],
                         mem_norm_g[l], w_mem_kv[l], w_up_gla[l], w_up_pool[l], w_up_xattn[l], w_o[l])
        x = x + rms_norm(y, mix_post_g[l])
        x = x + 0.5 * rms_norm(swiglu(rms_norm(x, ffn2_pre_g[l]), ffn2_w_in[l], ffn2_w_out[l]), ffn2_post_g[l])
        x = rms_norm(x, final_g[l])
    return x
```

```python
import numpy as np
from contextlib import ExitStack
import concourse.bass as bass
import concourse.mybir as mybir
from concourse.bass_utils import run_bass_kernel_spmd

F32 = mybir.dt.float32
BF16 = mybir.dt.bfloat16
AF = mybir.ActivationFunctionType
ALU = mybir.AluOpType

D = 1024
S_LEN = 4096
T = 512
NT = S_LEN // T
NB = T // 128
DFF = 2816
FC = DFF // 128
MEM = 256
INW = 7184
EPS = 1e-6
NSLOT = 5
SLOTW = 528
ENGS = ("pe", "act", "dve", "pool", "sp")
NDMASEM = 8
NSCR = 64


class Sched:
    def __init__(self):
        self.ops = {e: [] for e in ENGS}
        self.res = {}
        self.ndma = {e: 0 for e in ENGS}

    def op(self, eng, fn, reads=(), writes=(), dma=False):
        ops = self.ops
        idx = len(ops[eng])
        me = (eng, idx)
        deps = set()
        res = self.res
        for r in reads:
            st = res.get(r)
            if st is not None and st[0] is not None:
                deps.add(st[0])
        for w in writes:
            st = res.get(w)
            if st is not None:
                if st[0] is not None:
                    deps.add(st[0])
                deps.update(st[1])
        deps.discard(me)
        if eng == "pe":
            deps = {d for d in deps if d[0] != "pe"}
        rec = dict(fn=fn, deps=deps, dma=dma, marked=dma, dmaidx=None)
        if dma:
            rec["dmaidx"] = self.ndma[eng]
            self.ndma[eng] += 1
        ops[eng].append(rec)
        for r in reads:
            st = res.get(r)
            if st is None:
                st = res[r] = [None, []]
            if not dma:
                st[1] = [x for x in st[1] if x[0] != eng or ops[x[0]][x[1]]["dma"]]
            st[1].append(me)
        for w in writes:
            res[w] = [me, []]
        return me

    def finalize(self, sems, dmasems):
        for e in ENGS:
            for rec in self.ops[e]:
                for (de, di) in rec["deps"]:
                    self.ops[de][di]["marked"] = True
        for e in ENGS:
            cnt = 0
            for rec in self.ops[e]:
                if rec["dma"]:
                    k = rec["dmaidx"]
                    rec["tok"] = (dmasems[e][k % NDMASEM], 16 * (k // NDMASEM + 1))
                    rec["pre"] = (dmasems[e][k % NDMASEM], 16 * (k // NDMASEM))
                elif rec["marked"]:
                    cnt += 1
                    rec["tok"] = (sems[e], cnt)
        self.dmasems = dmasems

    def run_engine(self, e, h):
        waited = {}
        for rec in self.ops[e]:
            need = {}
            for (de, di) in rec["deps"]:
                s, v = self.ops[de][di]["tok"]
                if need.get(s, (None, 0))[1] < v:
                    need[s] = (s, v)
            if rec["dma"] and rec["pre"][1] > 0:
                s, v = rec["pre"]
                if need.get(s, (None, 0))[1] < v:
                    need[s] = (s, v)
            for s, (_, v) in need.items():
                if waited.get(s, 0) < v:
                    h.wait_ge(s, v)
                    waited[s] = v
            ins = rec["fn"](h)
            if rec["dma"]:
                ins.then_inc(rec["tok"][0], 16)
            elif rec["marked"]:
                ins.then_inc(rec["tok"][0], 1)
        for k in range(min(self.ndma[e], NDMASEM)):
            n = (self.ndma[e] - 1 - k) // NDMASEM + 1
            h.wait_ge(self.dmasems[e][k], 16 * n)


def build_nc(ntiles=NT, stages=("ffn1", "mix", "ffn2", "final")):
    nc = bass.Bass("TRN2", target_bir_lowering=False)

    def din(name, shape):
        return nc.dram_tensor(name, shape, F32, kind="ExternalInput").ap()

    x = din("x", [S_LEN, D])
    mem = din("mem", [MEM, D])
    w1_in = din("ffn1_w_in", [D, 2 * DFF])
    w1_out = din("ffn1_w_out", [DFF, D])
    w2_in = din("ffn2_w_in", [D, 2 * DFF])
    w2_out = din("ffn2_w_out", [DFF, D])
    w_in = din("w_in", [D, INW])
    w_fu = din("w_fu", [16, 512])
    b_f = din("b_f", [1, 512])
    w_pool = din("w_pool", [4, 128, 128])
    w_mem_kv = din("w_mem_kv", [D, D])
    w_up_gla = din("w_up_gla", [D, D])
    w_up_pool = din("w_up_pool", [512, D])
    w_up_xattn = din("w_up_xattn", [512, D])
    w_o = din("w_o", [D, D])
    gains = {n: din(n, [D]) for n in ("ffn1_pre_g", "ffn1_post_g", "mix_pre_g", "mix_post_g",
                                        "ffn2_pre_g", "ffn2_post_g", "final_g", "mem_norm_g")}
    gng_d = din("gng", [128, 8])
    psc_d = din("psc", [128, 4])
    c_ident = din("c_ident", [128, 128])
    c_emat = din("c_emat", [128, 128])
    c_ind = din("c_ind", [128, 2])
    c_invc = din("c_invc", [64])
    y = nc.dram_tensor("y", [S_LEN, D], F32, kind="ExternalOutput").ap()
    wscr = nc.dram_tensor("wscr", [NSCR, 128, 8 * SLOTW], BF16, kind="Internal").ap()

    S = Sched()
    with ExitStack() as es:
        def sb(name, shape, dt):
            return es.enter_context(nc.sbuf_tensor(name, shape, dt))

        xres = sb("xres", [128, 8, D], F32)
        gbuf = sb("gbuf", [128, 3, D], F32)
        hT = sb("hT", [128, 8, T], BF16)
        uT2 = sb("uT", [128, FC * T], BF16)
        uT = uT2[:].rearrange("p (j t) -> p j t", j=FC)
        oT = uT2[:, 0:16 * T].bitcast(F32).rearrange("p (j t) -> p j t", j=8)
        wring = sb("wring", [128, NSLOT, 8, SLOTW], BF16)
        hbuf = sb("hbuf", [128, 2, D], BF16)
        ssq = sb("ssq", [128, 8], F32)
        ssp = sb("ssp", [128, 2, NB], F32)
        rs = sb("rs", [128, 8], F32)
        satmp = sb("satmp", [128, 2, T], F32)
        tmpf = sb("tmpf", [128, 2, T], F32)
        identf = sb("identf", [128, 128], F32)
        identb = sb("identb", [128, 128], BF16)
        ones_bf = sb("ones_bf", [128, 128], BF16)
        emat = sb("emat", [128, 128], F32)
        ind = sb("ind", [128, 2], F32)
        dmy = sb("dmy", [128, 2], F32)
        gng = sb("gng_t", [128, 8], F32)
        psc = sb("psc_t", [128, 4], F32)
        invc = sb("invc", [128, 4, 16], F32)
        wpool = sb("wpool", [128, 4, 128], BF16)
        wfu_aug = sb("wfu_aug", [128, 512], F32)
        flow_aug = sb("flow_aug", [128, T], F32)
        KT = sb("KT", [128, 4, MEM], BF16)
        Vt = sb("Vt", [128, 2, 512], BF16)
        qT = sb("qT", [128, 4, T], BF16)
        kt_lo = sb("kt_lo", [128, NB, 512], BF16)
        kt_hi = sb("kt_hi", [128, NB, 512], BF16)
        v_flat = sb("v_tm", [128, NB * D], BF16)
        v_tm = v_flat[:].rearrange("p (b d) -> p b d", b=NB)
        ogT = v_flat[:].rearrange("p (j t) -> p j t", j=8)
        sgT = sb("sgT", [128, 8, T], BF16)
        xqT = sb("xqT", [128, 4, T], BF16)
        pT_ext = sb("pT_ext", [128, 4, 16 + T], F32)
        plA = sb("plA", [128, 4, 16 + T], F32)
        plB = sb("plB", [128, 4, 16 + T], F32)
        mixedT = sb("mixedT", [128, 4, T], BF16)
        la = tmpf[:, 0:1, :]
        multb = tmpf[:, 1:2, :]
        dec = sb("dec", [128, NB, 8], F32)
        state32 = sb("state32", [128, 4, 256], F32)
        state16 = sb("state16", [128, 1, 4, 256], BF16)
        PT = sb("PT", [128, 1, 2, T], BF16)
        sigt = satmp
        memnT = ogT[:, :, 0:MEM]
        macc = plA[:, :, 0:T]
        mergedT = sgT

        ps = [es.enter_context(nc.psum_tensor(f"ps{i}", [128, 512], F32)) for i in range(8)]
        sems = {e: es.enter_context(nc.semaphore("s_" + e)) for e in ENGS}
        dsems = {e: [es.enter_context(nc.semaphore(f"d_{e}{i}")) for i in range(NDMASEM)] for e in ("sp", "pool")}

        def dma(eng, out, in_, reads, writes):
            S.op(eng, lambda h: h.dma_start(out=out, in_=in_), reads, writes, dma=True)

        def act(out, in_, func, reads, writes, **kw):
            S.op("act", lambda h: h.activation(out=out, in_=in_, func=func, **kw), reads, writes)

        def stt(out, in0, scalar, in1, op0, op1, reads, writes):
            S.op("dve", lambda h: h.scalar_tensor_tensor(out=out, in0=in0, scalar=scalar, in1=in1, op0=op0, op1=op1),
                 reads, writes)

        def tt(out, in0, in1, op, reads, writes):
            S.op("dve", lambda h: h.tensor_tensor(out=out, in0=in0, in1=in1, op=op), reads, writes)

        def tsc(out, in0, s1, s2, op0, op1, reads, writes):
            if s2 is None:
                S.op("dve", lambda h: h.tensor_scalar(out=out, in0=in0, scalar1=s1, scalar2=None, op0=op0), reads, writes)
            else:
                S.op("dve", lambda h: h.tensor_scalar(out=out, in0=in0, scalar1=s1, scalar2=s2, op0=op0, op1=op1),
                     reads, writes)

        def vcopy(out, in_, reads, writes):
            S.op("dve", lambda h: h.tensor_copy(out=out, in_=in_), reads, writes)

        def vmemset(ap, val, writes):
            S.op("dve", lambda h: h.memset(ap, val), (), writes)

        def mm(out, lhsT, rhs, start, stop, reads, writes):
            S.op("pe", lambda h: h.matmul(out, lhsT=lhsT, rhs=rhs, start=start, stop=stop), reads, writes)

        def tr(out, in_, reads, writes):
            S.op("pe", lambda h: h.transpose(out=out, in_=in_, identity=identb[:]), reads, writes)

        bank_ctr = [0]
        bank_live = [False] * 8
        bank_rel = list(range(8))

        def bank():
            free = [b for b in range(8) if not bank_live[b]]
            if not free:
                raise AssertionError("no free PSUM bank")
            b = min(free, key=lambda q: bank_rel[q])
            bank_live[b] = True
            return b

        def rel(*bs):
            for b in bs:
                assert bank_live[b]
                bank_live[b] = False
                bank_ctr[0] += 1
                bank_rel[b] = 8 + bank_ctr[0]

        slab_ctr = [0]

        scr_index = {}
        scr_on = [False]

        def load_w(W, k0, nk, c0, ncols):
            s = slab_ctr[0] % NSLOT
            slab_ctr[0] += 1
            view = wring[:, s, 0:nk, 0:ncols]
            key = (id(W), k0, nk, c0, ncols)
            if key in scr_index:
                i = scr_index[key]
                src = wscr[i, :, 0:nk * ncols].rearrange("p (k n) -> p k n", k=nk)
                dma("pool", view, src, [("scr", i)], [("w", s)])
                return wring, s
            src = W[k0 * 128:(k0 + nk) * 128, c0:c0 + ncols].rearrange("(k p) n -> p k n", p=128)
            dma("pool", view, src, (), [("w", s)])
            if scr_on[0] and ntiles > 1:
                i = len(scr_index)
                assert i < NSCR
                scr_index[key] = i
                dst = wscr[i, :, 0:nk * ncols].rearrange("p (k n) -> p k n", k=nk)
                dma("sp", dst, view, [("w", s)], [("scr", i)])
            return wring, s

        UT16 = [("uT", j) for j in range(16)]
        PSK = lambda b: ("ps", b)
        HT_ALL = [("hT", b) for b in range(NB)]

        dma("sp", identf[:], c_ident, (), ["identf"])
        dma("sp", emat[:], c_emat, (), ["emat"])
        dma("sp", ind[:], c_ind, (), ["ind"])
        dma("sp", gng[:], gng_d, (), ["gng"])
        dma("sp", psc[:], psc_d, (), ["psc"])
        dma("sp", invc[:].rearrange("p g t -> p (g t)"), c_invc.partition_broadcast(128), (), ["invc"])
        gctr = [0]

        def load_gain(name):
            i = gctr[0] % 3
            gctr[0] += 1
            dma("sp", gbuf[:, i, :], gains[name].partition_broadcast(128), (), [("g", i)])
            return gbuf[:, i, :], ("g", i)
        vcopy(identb[:], identf[:], ["identf"], ["identb"])
        vmemset(ones_bf[:], 1.0, ["ones"])
        vmemset(dmy[:], 1.0, ["dmy0", "dmy1"])

        def table_prefetch():
            act(dmy[:, 1:2], dmy[:, 0:1], AF.Ln, ["dmy0"], ["dmy1"])
        vmemset(wfu_aug[:], 0.0, ["wfu"])
        vmemset(flow_aug[:], 0.0, ["flow"])
        vmemset(flow_aug[32:33, :], 1.0, ["flow"])
        dma("sp", wfu_aug[0:16, :], w_fu, (), ["wfu"])
        dma("sp", wfu_aug[32:33, :], b_f, (), ["wfu"])
        dma("pool", wpool[:], w_pool.rearrange("g c d -> c g d"), (), ["wpool"])
        vmemset(pT_ext[:, :, 0:16], 0.0, ["pT"])
        vmemset(state32[:], 0.0, ["state32"])
        vmemset(kt_lo[:], 0.0, [("kt", b) for b in range(NB)])
        vmemset(kt_hi[:], 0.0, [("kt", b) for b in range(NB)])

        def rstd_chain(n, scale, c0=0):
            rk = [("rs", i) for i in range(c0, c0 + n)]
            tsc(rs[:, c0:c0 + n], ssq[:, c0:c0 + n], scale, EPS, ALU.mult, ALU.add,
                [("ssq", i) for i in range(c0, c0 + n)], rk)
            act(rs[:, c0:c0 + n], rs[:, c0:c0 + n], AF.Ln, rk, rk)
            act(rs[:, c0:c0 + n], rs[:, c0:c0 + n], AF.Exp, rk, rk, scale=-0.5)

        def transpose_pe(src2d, srckey):
            bks = [bank(), bank()]
            for c in range(8):
                bk = bks[c // 4]
                mm(ps[bk][:, (c % 4) * 128:(c % 4 + 1) * 128], src2d[:, c * 128:(c + 1) * 128], identb[:], True, True,
                   [srckey, "identb"], [PSK(bk)])
            return bks

        def transpose_copy(bks, dst, b, dstkeys):
            for i2 in range(2):
                act(dst[:, 4 * i2:4 * i2 + 4, b * 128:(b + 1) * 128],
                    ps[bks[i2]][:].rearrange("p (c t) -> p c t", c=4), AF.Copy, [PSK(bks[i2])], dstkeys)
            rel(*bks)

        def transpose_block(src2d, srckey, dst, b, dstkeys):
            transpose_copy(transpose_pe(src2d, srckey), dst, b, dstkeys)

        if "mix" in stages:
            for b in range(2):
                dma("sp", xres[:, b, :], mem[b * 128:(b + 1) * 128, :], (), [("x", b)])
            for b in range(2):
                act(hbuf[:, b, :], xres[:, b, :], AF.Square, [("x", b)], [("ssq", b), ("hb", b)], accum_out=ssq[:, b:b + 1])
            rstd_chain(2, 1.0 / D)
            gm, gmk = load_gain("mem_norm_g")
            OG_ALL = [("og", k) for k in range(8)] + [("v", b) for b in range(NB)]
            for b in range(2):
                stt(hbuf[:, b, :], xres[:, b, :], rs[:, b:b + 1], gm, ALU.mult, ALU.mult,
                    [("x", b), ("rs", b), gmk], [("hb", b)])
                transpose_block(hbuf[:, b, :], ("hb", b), memnT, b, OG_ALL)
            wr, s = load_w(w_mem_kv, 0, 8, 0, 512)
            for hh in range(4):
                bk = bank()
                for k in range(8):
                    mm(ps[bk][:, 0:MEM], wr[:, s, k, hh * 128:(hh + 1) * 128], memnT[:, k, :], k == 0, k == 7,
                       [("w", s)] + OG_ALL, [PSK(bk)])
                act(KT[:, hh, :], ps[bk][:, 0:MEM], AF.Copy, [PSK(bk)], ["KT"])
                rel(bk)
            wr, s = load_w(w_mem_kv, 0, 8, 512, 512)
            for blk in range(2):
                bk = bank()
                for k in range(8):
                    mm(ps[bk][:], memnT[:, k, blk * 128:(blk + 1) * 128], wr[:, s, k, 0:512], k == 0, k == 7,
                       [("w", s)] + OG_ALL, [PSK(bk)])
                act(Vt[:, blk, :], ps[bk][:], AF.Copy, [PSK(bk)], ["Vt"])
                rel(bk)

        def xslot(t, b):
            return 4 * (t % 2) + b

        def XK(t, b):
            return ("x", xslot(t, b))

        def prenorm_stats(t, blocks, coff):
            for b in blocks:
                act(hbuf[:, b % 2, :], xres[:, xslot(t, b), :], AF.Square, [XK(t, b)],
                    [("ssq", b + coff), ("hb", b % 2)], accum_out=ssq[:, b + coff:b + coff + 1])
            rstd_chain(len(blocks), 1.0 / D, blocks[0] + coff)

        def prenorm_scale(t, blocks, coff, gap, gkey):
            for b in blocks:
                hb = b % 2
                stt(hbuf[:, hb, :], xres[:, xslot(t, b), :], rs[:, b + coff:b + coff + 1], gap, ALU.mult, ALU.mult,
                    [XK(t, b), ("rs", b + coff), gkey], [("hb", hb)])

        def prenorm_transpose(blocks):
            prenorm_transpose_copy(prenorm_transpose_pe(blocks))

        def prenorm_transpose_pe(blocks):
            return [(b, transpose_pe(hbuf[:, b % 2, :], ("hb", b % 2))) for b in blocks]

        def prenorm_transpose_copy(pend):
            for b, bks in pend:
                transpose_copy(bks, hT, b, [("hT", b)])

        SA_FLAT = satmp[:].rearrange("p a b -> p (a b)")

        def out_and_post(t, W, src, srckey, nkc, post_gname, coef, tail, parts, resident=False, lag=1, a_lag=0):
            gap, gkey = load_gain(post_gname)
            res = None
            if resident:
                assert nkc <= 8
                res = [load_w(W, 0, nkc, n * 512, 512) for n in range(2)]
            for pi, blocks in enumerate(parts):
                banks = {}
                for n in range(2):
                    for b in blocks:
                        banks[(n, b)] = bank()
                for n in range(2):
                    k0 = 0
                    while k0 < nkc:
                        nk = min(8, nkc - k0)
                        wr, sl = res[n] if resident else load_w(W, k0, nk, n * 512, 512)
                        for kk in range(nk):
                            k = k0 + kk
                            for b in blocks:
                                bk = banks[(n, b)]
                                mm(ps[bk][:], src[:, k, b * 128:(b + 1) * 128], wr[:, sl, kk, 0:512], k == 0,
                                   k == nkc - 1, [("w", sl), (srckey, k)], [PSK(bk)])
                        k0 += nk
                tail(pi, blocks, "M")
                if pi >= lag:
                    tail(pi - lag, parts[pi - lag], "B1")
                i = 0
                for n in range(2):
                    for b in blocks:
                        bk = banks[(n, b)]
                        act(satmp[:, i % 2, :], ps[bk][:], AF.Square, [PSK(bk)], [("ssp", n, b), ("sa", i % 2)],
                            accum_out=ssp[:, n, b:b + 1])
                        i += 1
                b0, nb = blocks[0], len(blocks)
                tt(ssq[:, b0:b0 + nb], ssp[:, 0, b0:b0 + nb], ssp[:, 1, b0:b0 + nb], ALU.add,
                   [("ssp", n, b) for n in range(2) for b in blocks], [("ssq", b) for b in blocks])
                rstd_chain(nb, 1.0 / D, b0)
                tail(pi, blocks, "M2")
                if pi >= lag:
                    tail(pi - lag, parts[pi - lag], "B2")
                i = 0
                for b in blocks:
                    xs = xslot(t, b)
                    for n in range(2):
                        bk = banks[(n, b)]
                        stt(tmpf[:, i % 2, :], ps[bk][:], rs[:, b:b + 1], gap[:, n * 512:(n + 1) * 512],
                            ALU.mult, ALU.mult, [PSK(bk), ("rs", b), gkey], [("tmpf", i % 2)])
                        stt(xres[:, xs, n * 512:(n + 1) * 512], tmpf[:, i % 2, :], coef, xres[:, xs, n * 512:(n + 1) * 512],
                            ALU.mult, ALU.add, [XK(t, b), ("tmpf", i % 2)], [XK(t, b)])
                        rel(bk)
                        i += 1
                if pi >= a_lag:
                    tail(pi - a_lag, parts[pi - a_lag], "A")
            np_ = len(parts)
            done_b = max(0, np_ - lag)
            for pa in range(max(0, np_ - a_lag), np_):
                while done_b < np_ and done_b <= pa - 2:
                    tail(done_b, parts[done_b], "B1")
                    tail(done_b, parts[done_b], "B2")
                    done_b += 1
                tail(pa, parts[pa], "A")
            for pi in range(done_b, np_):
                tail(pi, parts[pi], "B1")
                tail(pi, parts[pi], "B2")

        FFN_PARTS = [(0, 1), (2, 3)]
        WO_PARTS = [(0,), (1,), (2,), (3,)]

        def ffn_hidden(Win, pre=None):
            for j0 in range(0, FC, 4):
                nj = min(4, FC - j0)
                if pre is not None and j0 == 0:
                    (wrA, sA), (wrB, sB) = pre
                else:
                    wrA, sA = load_w(Win, 0, 8, j0 * 128, nj * 128)
                    wrB, sB = load_w(Win, 0, 8, DFF + j0 * 128, nj * 128)
                for jj in range(nj):
                    j = j0 + jj
                    bA = bank()
                    for k in range(8):
                        mm(ps[bA][:], wrA[:, sA, k, jj * 128:(jj + 1) * 128], hT[:, k, :], k == 0, k == 7,
                           [("w", sA)] + HT_ALL, [PSK(bA)])
                    bB = bank()
                    for k in range(8):
                        mm(ps[bB][:], wrB[:, sB, k, jj * 128:(jj + 1) * 128], hT[:, k, :], k == 0, k == 7,
                           [("w", sB)] + HT_ALL, [PSK(bB)])
                    act(satmp[:, j % 2, :], ps[bA][:], AF.Silu, [PSK(bA)], [("sa", j % 2)])
                    tt(uT[:, j, :], satmp[:, j % 2, :], ps[bB][:], ALU.mult, [("sa", j % 2), PSK(bB)], [("uT", j)])
                    rel(bA, bB)
            table_prefetch()

        def make_v_filler():
            fs = {}

            def pe():
                fs["wrv"] = [load_w(w_in, 0, 8, 1024 + vh * 512, 512) for vh in range(2)]
                fs["banks"] = []
                for b in (0, 1):
                    for vh in range(2):
                        wr, sv = fs["wrv"][vh]
                        bkv = bank()
                        for k in range(8):
                            mm(ps[bkv][:], hT[:, k, b * 128:(b + 1) * 128], wr[:, sv, k, 0:512], k == 0, k == 7,
                               [("w", sv), ("hT", b)], [PSK(bkv)])
                        fs["banks"].append((b, vh, bkv))

            def evac():
                for b, vh, bkv in fs["banks"]:
                    act(v_tm[:, b, vh * 512:(vh + 1) * 512], ps[bkv][:], AF.Copy, [PSK(bkv)], [("v", b)])
                    rel(bkv)
            return fs, pe, evac

        def wo_phase(t):
            gap, gkey = load_gain("mix_post_g")
            gpre = load_gain("ffn2_pre_g")
            res = [load_w(w_o, 0, 8, n * 512, 512) for n in range(2)]
            pre = (load_w(w2_in, 0, 8, 0, 512), load_w(w2_in, 0, 8, DFF, 512))
            banks = {}

            def MM(b):
                for n in range(2):
                    banks[(n, b)] = bank()
                for n in range(2):
                    wr, sl = res[n]
                    bk = banks[(n, b)]
                    for k in range(8):
                        mm(ps[bk][:], mergedT[:, k, b * 128:(b + 1) * 128], wr[:, sl, k, 0:512], k == 0, k == 7,
                           [("w", sl), ("sg", k)], [PSK(bk)])

            def S1(b):
                for n in range(2):
                    bk = banks[(n, b)]
                    act(satmp[:, n, :], ps[bk][:], AF.Square, [PSK(bk)], [("ssp", n, b), ("sa", n)],
                        accum_out=ssp[:, n, b:b + 1])
                tt(ssq[:, b:b + 1], ssp[:, 0, b:b + 1], ssp[:, 1, b:b + 1], ALU.add,
                   [("ssp", 0, b), ("ssp", 1, b)], [("ssq", b)])
                rstd_chain(1, 1.0 / D, b)

            def S2(b):
                xs = xslot(t, b)
                for n in range(2):
                    bk = banks[(n, b)]
                    stt(tmpf[:, n, :], ps[bk][:], rs[:, b:b + 1], gap[:, n * 512:(n + 1) * 512],
                        ALU.mult, ALU.mult, [PSK(bk), ("rs", b), gkey], [("tmpf", n)])
                    S.op("pool", lambda h, o=xres[:, xs, n * 512:(n + 1) * 512], i1=tmpf[:, n, :]:
                         h.tensor_tensor(out=o, in0=o, in1=i1, op=ALU.add), [XK(t, b), ("tmpf", n)], [XK(t, b)])
                    rel(bk)

            def S3(b):
                act(SA_FLAT, xres[:, xslot(t, b), :], AF.Square, [XK(t, b)], [("ssq", b), ("sa", 0), ("sa", 1)],
                    accum_out=ssq[:, b:b + 1])
                rstd_chain(1, 1.0 / D, b)

            def S4(b):
                prenorm_scale(t, (b,), 0, *gpre)

            def B(b):
                prenorm_transpose((b,))

            MM(0); S1(0)
            MM(1); S1(1); S2(0)
            MM(2); S1(2); S2(1); S3(0)
            MM(3); S1(3); S2(2); S3(1); S4(0)
            S2(3); S3(2); S4(1); B(0)
            S3(3); S4(2); B(1)
            S4(3); B(2); B(3)
            return pre

        def make_prenorm_tail(t, gname, filler=None, nparts=2):
            st = {}

            def tail(pi, blocks, phase):
                if phase in ("M", "M2"):
                    return
                if phase == "B1":
                    st[("p", pi)] = prenorm_transpose_pe(blocks)
                    return
                if phase == "B2":
                    prenorm_transpose_copy(st.pop(("p", pi)))
                    if filler is not None and pi == 0:
                        filler[1]()
                    return
                if pi == 0:
                    st["g"] = load_gain(gname)
                prenorm_stats(t, blocks, 0)
                prenorm_scale(t, blocks, 0, *st["g"])
                if filler is not None and pi == nparts - 1:
                    filler[2]()
            return tail

        def mixer(t, vfill=None):
            sc = 128.0 ** -0.5
            wr, s = load_w(w_in, 0, 8, 3072, 528)
            bk = bank()
            for k in range(8):
                mm(ps[bk][:], wr[:, s, k, 0:128], hT[:, k, :], k == 0, k == 7, [("w", s)] + HT_ALL, [PSK(bk)])
            vcopy(flow_aug[0:16, :], ps[bk][0:16, :], [PSK(bk)], ["flow"])
            rel(bk)
            for g in range(4):
                bk = bank()
                for k in range(8):
                    mm(ps[bk][:], wr[:, s, k, 16 + g * 128:16 + (g + 1) * 128], hT[:, k, :], k == 0, k == 7,
                       [("w", s)] + HT_ALL, [PSK(bk)])
                act(pT_ext[:, g, 16:16 + T], ps[bk][:], AF.Copy, [PSK(bk)], [("pT", g)])
                rel(bk)
            wrk, sk = load_w(w_in, 0, 8, 512, 512)
            wrv = vfill["wrv"]
            qslab = load_w(w_in, 0, 8, 0, 512)
            xqslab = [None]

            def qproj(which, chunks):
                wr, s_ = qslab if which == "q" else xqslab[0]
                dst, key = (qT, "qT") if which == "q" else (xqT, "xq")
                for hh in chunks:
                    bkq = bank()
                    for k in range(8):
                        mm(ps[bkq][:], wr[:, s_, k, hh * 128:(hh + 1) * 128], hT[:, k, :], k == 0, k == 7,
                           [("w", s_)] + HT_ALL, [PSK(bkq)])
                    act(dst[:, hh, :], ps[bkq][:], AF.Identity, [PSK(bkq)], [(key, hh)], scale=sc)
                    rel(bkq)

            for b in (2, 3, 0, 1):
                pb = 0
                if b == 1:
                    xqslab[0] = load_w(w_in, 0, 8, 3600, 512)
                bk = bank()
                mm(ps[bk][:], flow_aug[:, b * 128:(b + 1) * 128], wfu_aug[:], True, True, ["flow", "wfu"], [PSK(bk)])
                act(la[:, pb, :], ps[bk][:], AF.Exp, [PSK(bk)], [("tmpf", 0)], scale=-1.0)
                rel(bk)
                act(la[:, pb, :], la[:, pb, :], AF.Ln, [("tmpf", 0)], [("tmpf", 0)], bias=1.0)
                if b < 2:
                    qproj("q" if b == 0 else "xq", (0, 1))
                for vh in range(2):
                    if b < 2:
                        continue
                    wr, sv = wrv[vh]
                    bkv = bank()
                    for k in range(8):
                        mm(ps[bkv][:], hT[:, k, b * 128:(b + 1) * 128], wr[:, sv, k, 0:512], k == 0, k == 7,
                           [("w", sv), ("hT", b)], [PSK(bkv)])
                    act(v_tm[:, b, vh * 512:(vh + 1) * 512], ps[bkv][:], AF.Copy, [PSK(bkv)], [("v", b)])
                    rel(bkv)
                bk = bank()
                mm(ps[bk][:], emat[:], la[:, pb, :], True, True, ["emat", ("tmpf", 0)], [PSK(bk)])
                act(multb[:, pb, :], ps[bk][:], AF.Exp, [PSK(bk)], [("tmpf", 1)])
                rel(bk)
                bk2 = bank()
                for hh in range(4):
                    mm(ps[bk2][:, hh * 2:(hh + 1) * 2], la[:, pb, hh * 128:(hh + 1) * 128], ind[:], True, True,
                       ["ind", ("tmpf", 0)], [PSK(bk2)])
                act(dec[:, b, :], ps[bk2][:, 0:8], AF.Exp, [PSK(bk2)], [("dec", b)])
                rel(bk2)
                if b < 2:
                    qproj("q" if b == 0 else "xq", (2, 3))
                bk3 = bank()
                for k in range(8):
                    mm(ps[bk3][:], hT[:, k, b * 128:(b + 1) * 128], wrk[:, sk, k, 0:512], k == 0, k == 7,
                       [("w", sk), ("hT", b)], [PSK(bk3)])
                tt(kt_lo[0:64, b, :], ps[bk3][0:64, :], multb[0:64, pb, :], ALU.mult, [PSK(bk3), ("tmpf", 1)],
                   [("kt", b)])
                tt(kt_hi[64:128, b, :], ps[bk3][64:128, :], multb[64:128, pb, :], ALU.mult, [PSK(bk3), ("tmpf", 1)],
                   [("kt", b)])
                rel(bk3)
            for gh in range(2):
                wr, s = load_w(w_in, 0, 8, 2048 + gh * 512, 512)
                for jj in range(4):
                    bk = bank()
                    for k in range(8):
                        mm(ps[bk][:], wr[:, s, k, jj * 128:(jj + 1) * 128], hT[:, k, :], k == 0, k == 7,
                           [("w", s)] + HT_ALL, [PSK(bk)])
                    act(sgT[:, gh * 4 + jj, :], ps[bk][:], AF.Silu, [PSK(bk)], [("sg", gh * 4 + jj)])
                    rel(bk)
            table_prefetch()
            PT_ALL = [("pT", g) for g in range(4)]
            L = 16 + T
            tt(plA[:, :, 1:L], pT_ext[:, :, 1:L], pT_ext[:, :, 0:L - 1], ALU.add, PT_ALL, ["plA"])
            tt(plB[:, 1:4, 3:L], plA[:, 1:4, 3:L], plA[:, 1:4, 1:L - 2], ALU.add, ["plA"], ["plB"])
            tt(plA[:, 2:4, 7:L], plB[:, 2:4, 7:L], plB[:, 2:4, 3:L - 4], ALU.add, ["plB", "plA"], ["plA"])
            tt(plB[:, 3, 15:L], plA[:, 3, 15:L], plA[:, 3, 7:L - 8], ALU.add, ["plA", "plB"], ["plB"])
            for g in range(4):
                src = plA if g in (0, 2) else plB
                w = float(2 ** (g + 1))
                stt(mixedT[:, g, :], src[:, g, 16:L], 1.0 / w, pT_ext[:, g, 16:L], ALU.mult, ALU.subtract,
                    ["plA", "plB", ("pT", g)], [("mixed", g)])
                if t == 0:
                    tt(tmpf[:, 0, 0:16], src[:, g, 16:32], invc[:, g, :], ALU.mult, ["plA", "plB", "invc"], [("tmpf", 0)])
                    tt(mixedT[:, g, 0:16], tmpf[:, 0, 0:16], pT_ext[:, g, 16:32], ALU.subtract,
                       [("tmpf", 0), ("pT", g)], [("mixed", g)])
            for g in range(4):
                bk = bank()
                mm(ps[bk][:], wpool[:, g, :], mixedT[:, g, :], True, True, ["wpool", ("mixed", g)], [PSK(bk)])
                act(mixedT[:, g, :], ps[bk][:], AF.Identity, [PSK(bk)], [("mixed", g)], scale=psc[:, g:g + 1])
                rel(bk)
            vcopy(pT_ext[:, :, 0:16], pT_ext[:, :, T:T + 16], PT_ALL, PT_ALL)

            def xattn_scores(hh):
                pp = 0
                for mc in range(2):
                    bk = bank()
                    mm(ps[bk][:], KT[:, hh, mc * 128:(mc + 1) * 128], xqT[:, hh, :], True, True, ["KT", ("xq", hh)],
                       [PSK(bk)])
                    act(PT[:, pp, mc, :], ps[bk][:], AF.Exp, [PSK(bk)], [("PT", pp, mc)])
                    rel(bk)

            def xattn_pv(hh):
                pp = 0
                tp_ = hh % 2
                bd = bank()
                for mc in range(2):
                    mm(ps[bd][:], ones_bf[:], PT[:, pp, mc, :], mc == 0, mc == 1, ["ones", ("PT", pp, mc)], [PSK(bd)])
                bv = bank()
                for mc in range(2):
                    mm(ps[bv][:], Vt[:, mc, hh * 128:(hh + 1) * 128], PT[:, pp, mc, :], mc == 0, mc == 1,
                       ["Vt", ("PT", pp, mc)], [PSK(bv)])
                act(ps[bd][:], ps[bd][:], AF.Ln, [PSK(bd)], [PSK(bd)])
                act(tmpf[:, tp_, :], ps[bd][:], AF.Exp, [PSK(bd)], [("tmpf", tp_)], scale=-1.0)
                tt(xqT[:, hh, :], ps[bv][:], tmpf[:, tp_, :], ALU.mult, [PSK(bv), ("tmpf", tp_)], [("xq", hh)])
                rel(bd, bv)

            for hh in range(4):
                xattn_scores(hh)
                xattn_pv(hh)
            act(dmy[:, 1:2], dmy[:, 0:1], AF.Sigmoid, ["dmy0"], ["dmy1"])

            ups = [(w_up_gla, ogT, "og", 8), (w_up_pool, mixedT, "mixed", 4), (w_up_xattn, xqT, "xq", 4)]

            def merge_slabs(i, jg):
                Wu, src, skey, nk = ups[i]
                wrg, sg_ = load_w(w_in, 0, 8, 4112 + i * 1024 + jg * 512, 512)
                wru, su = load_w(Wu, 0, nk, jg * 512, 512)
                return (wrg, sg_, wru, su)

            def merge_unit(step, i, jg, jj, slabs):
                merge_evac(*merge_mm(step, i, jg, jj, slabs))

            def merge_mm(step, i, jg, jj, slabs):
                Wu, src, skey, nk = ups[i]
                wrg, sg_, wru, su = slabs
                j = jg * 4 + jj
                mbuf = plA if jg == 0 else plB
                mk = "plA" if jg == 0 else "plB"
                macc_j = mbuf[:, jj, 0:T]
                bz = bank()
                for k in range(8):
                    mm(ps[bz][:], wrg[:, sg_, k, jj * 128:(jj + 1) * 128], hT[:, k, :], k == 0, k == 7,
                       [("w", sg_)] + HT_ALL, [PSK(bz)])
                by = bank()
                for k in range(nk):
                    mm(ps[by][:], wru[:, su, k, jj * 128:(jj + 1) * 128], src[:, k, :], k == 0, k == nk - 1,
                       [("w", su), (skey, k)] + ([("v", k // 2)] if i == 0 else []), [PSK(by)])
                return (step, i, jj, j, macc_j, mk, bz, by)

            def merge_evac(step, i, jj, j, macc_j, mk, bz, by):
                sp_ = (i * 4 + jj) % 2
                act(sigt[:, sp_, :], ps[bz][:], AF.Sigmoid, [PSK(bz)], [("sa", sp_)])
                rel(bz)
                if step == 0:
                    tt(macc_j, sigt[:, sp_, :], ps[by][:], ALU.mult, [("sa", sp_), PSK(by), mk], [("macc", j), mk])
                else:
                    tt(tmpf[:, sp_, :], sigt[:, sp_, :], ps[by][:], ALU.mult, [("sa", sp_), PSK(by)], [("tmpf", sp_)])
                    if step == 1:
                        tt(macc_j, macc_j, tmpf[:, sp_, :], ALU.add, [("macc", j), ("tmpf", sp_), mk], [("macc", j), mk])
                    else:
                        tt(mergedT[:, j, :], macc_j, tmpf[:, sp_, :], ALU.add, [("macc", j), ("tmpf", sp_), mk],
                           [("sg", j)])
                rel(by)

            def gla_U(c):
                b = c // 2
                ktx = kt_lo if c % 2 == 0 else kt_hi
                bus = [bank(), bank()]
                for hh in range(4):
                    bu = bus[hh // 2]
                    col = (hh % 2) * 256
                    mm(ps[bu][:, col:col + 256], ktx[:, b, hh * 128:(hh + 1) * 128], v_tm[:, b, hh * 256:(hh + 1) * 256],
                       True, True, [("kt", b), ("v", b)], [PSK(bu)])
                return bus

            def gla_rest(c, bus, hook=None):
                b = c // 2
                half = c % 2
                for hh in range(4):
                    bu = bus[hh // 2]
                    col = (hh % 2) * 256
                    stt(state32[:, hh, :], state32[:, hh, :], dec[:, b, hh * 2 + half:hh * 2 + half + 1],
                        ps[bu][:, col:col + 256], ALU.mult, ALU.add, [("st32", hh), ("dec", b), PSK(bu)], [("st32", hh)])
                    act(state16[:, 0, hh, :], state32[:, hh, :], AF.Copy, [("st32", hh)], [("st16", hh)])
                rel(*bus)
                if hook is not None:
                    hook()
                bo = bank()
                for hh in range(4):
                    for e in range(2):
                        mm(ps[bo][:, (hh * 2 + e) * 64:(hh * 2 + e + 1) * 64], state16[:, 0, hh, e * 128:(e + 1) * 128],
                           qT[:, hh, c * 64:(c + 1) * 64], True, True, [("st16", hh), ("qT", hh)], [PSK(bo)])
                act(oT[:, :, c * 64:(c + 1) * 64], ps[bo][:].rearrange("p (j t) -> p j t", j=8), AF.Copy, [PSK(bo)],
                    [("oT", c)] + UT16)
                rel(bo)

            bus_next = gla_U(0)
            pending = [None]
            for c in range(8):
                bus_cur = bus_next
                if c < 7:
                    bus_next = gla_U(c + 1)
                pend = pending[0]
                gla_rest(c, bus_cur, (lambda p=pend: merge_evac(*p)) if pend is not None else None)
                if c % 4 == 0:
                    pslabs = merge_slabs(1, c // 4)
                pending[0] = merge_mm(0, 1, c // 4, c % 4, pslabs)
            merge_evac(*pending[0])
            OT_ALL = [("oT", c) for c in range(8)] + UT16
            xslabs0 = merge_slabs(2, 0)
            pend0 = merge_mm(1, 2, 0, 0, xslabs0)

            def sqbuf(hh, e):
                if hh < 2:
                    return qT[:, hh * 2 + e, :], ("qT", hh * 2 + e)
                return mixedT[:, (hh - 2) * 2 + e, :], ("mixed", (hh - 2) * 2 + e)

            for hh in range(4):
                for e in range(2):
                    sq_ap, sq_key = sqbuf(hh, e)
                    act(sq_ap, oT[:, hh * 2 + e, :], AF.Square, OT_ALL, [sq_key])
            obanks = []
            for hh in range(4):
                bk = bank()
                obanks.append(bk)
                for e in range(2):
                    sq_ap, sq_key = sqbuf(hh, e)
                    mm(ps[bk][:], ones_bf[:], sq_ap, e == 0, e == 1, ["ones", sq_key], [PSK(bk)])
            for hh in range(4):
                bk = obanks[hh]
                act(ps[bk][:], ps[bk][:], AF.Ln, [PSK(bk)], [PSK(bk)], scale=1.0 / 256, bias=EPS)
                act(ps[bk][:], ps[bk][:], AF.Exp, [PSK(bk)], [PSK(bk)], scale=-0.5)
            for hh in range(4):
                bk = obanks[hh]
                for e in range(2):
                    he = hh * 2 + e
                    stt(tmpf[:, e, :], oT[:, he, :], gng[:, he:he + 1], ps[bk][:], ALU.mult, ALU.mult,
                        OT_ALL + ["gng", PSK(bk)], [("tmpf", e)])
                    tt(ogT[:, he, :], tmpf[:, e, :], sgT[:, he, :], ALU.mult, [("tmpf", e), ("sg", he)],
                       [("og", he), ("v", he // 2)])
                rel(bk)
            merge_evac(*pend0)
            for jg in range(2):
                for step, i in ((1, 2), (2, 0)):
                    slabs = xslabs0 if (jg, i) == (0, 2) else merge_slabs(i, jg)
                    for jj in range(4):
                        if (jg, i, jj) == (0, 2, 0):
                            continue
                        merge_unit(step, i, jg, jj, slabs)
            table_prefetch()
            return wo_phase(t)

        def load_x(t, blocks):
            for b in blocks:
                r0 = (t * NB + b) * 128
                dma("sp", xres[:, xslot(t, b), :], x[r0:r0 + 128, :], (), [XK(t, b)])

        def make_final_tail(t):
            st = {}
            last = (t + 1 >= ntiles)

            def final_blocks(blocks):
                gfa, gfk = st["gf"]
                if "final" in stages:
                    for b in blocks:
                        act(SA_FLAT, xres[:, xslot(t, b), :], AF.Square, [XK(t, b)],
                            [("ssq", b), ("sa", 0), ("sa", 1)], accum_out=ssq[:, b:b + 1])
                    rstd_chain(len(blocks), 1.0 / D, blocks[0])
                for b in blocks:
                    xs = xslot(t, b)
                    r0 = (t * NB + b) * 128
                    if "final" in stages:
                        stt(xres[:, xs, :], xres[:, xs, :], rs[:, b:b + 1], gfa, ALU.mult, ALU.mult,
                            [XK(t, b), ("rs", b), gfk], [XK(t, b)])
                    dma("sp", y[r0:r0 + 128, :], xres[:, xs, :], [XK(t, b)], ())

            def pre():
                st["gf"] = load_gain("final_g")
                if not last:
                    st["gn"] = load_gain("ffn1_pre_g")
                    prenorm_stats(t + 1, (0, 1), 4)
                    prenorm_scale(t + 1, (0, 1), 4, *st["gn"])

            def tail(pi, blocks, phase):
                if last:
                    if phase == "A":
                        final_blocks(blocks)
                    return
                if phase == "M":
                    prenorm_transpose((0, 1) if pi == 0 else (2, 3))
                elif phase == "A":
                    if pi == 0:
                        prenorm_stats(t + 1, (2, 3), 4)
                        prenorm_scale(t + 1, (2, 3), 4, *st["gn"])
                    final_blocks(blocks)
            return tail, pre

        scr_on[0] = True
        load_x(0, (0, 1, 2, 3))
        g0 = load_gain("ffn1_pre_g")
        for hf in range(2):
            prenorm_stats(0, (2 * hf, 2 * hf + 1), 0)
            prenorm_scale(0, (2 * hf, 2 * hf + 1), 0, *g0)
            prenorm_transpose((2 * hf, 2 * hf + 1))
        for t in range(ntiles):
            ffn_hidden(w1_in)
            vf = make_v_filler()
            out_and_post(t, w1_out, uT, "uT", FC, "ffn1_post_g", 0.5,
                         make_prenorm_tail(t, "mix_pre_g", filler=vf, nparts=len(FFN_PARTS)), FFN_PARTS)
            if t + 1 < ntiles:
                load_x(t + 1, (0, 1, 2, 3))
            pre2 = mixer(t, vfill=vf[0])
            ffn_hidden(w2_in, pre=pre2)
            ftail, fpre = make_final_tail(t)
            fpre()
            out_and_post(t, w2_out, uT, "uT", FC, "ffn2_post_g", 0.5, ftail, FFN_PARTS)

        S.finalize(sems, dsems)
        with nc.Block() as block:
            @block.sync
            def _(h):
                S.run_engine("sp", h)

            @block.gpsimd
            def _(h):
                S.run_engine("pool", h)

            @block.scalar
            def _(h):
                S.run_engine("act", h)

            @block.vector
            def _(h):
                S.run_engine("dve", h)

            @block.tensor
            def _(h):
                S.run_engine("pe", h)
    return nc, S


def _consts():
    ident = np.eye(128, dtype=np.float32)
    sidx = np.arange(128)
    same = (sidx[:, None] // 64) == (sidx[None, :] // 64)
    emat = np.where(same & (sidx[:, None] > sidx[None, :]), -1.0 / 16.0, 0.0).astype(np.float32)
    ind = np.where((sidx[:, None] // 64) == np.arange(2)[None, :], -1.0 / 16.0, 0.0).astype(np.float32)
    invc = np.zeros((4, 16), np.float32)
    for g in range(4):
        w = 2 ** (g + 1)
        invc[g] = 1.0 / np.minimum(np.arange(1, 17), w)
    return ident, emat, ind, invc.reshape(64)


def make_in_map(inputs, bidx):
    f = lambda a: np.ascontiguousarray(np.asarray(a, dtype=np.float32))
    ident, emat, ind, invc = _consts()
    m = {
        "x": f(inputs["x"][bidx]), "mem": f(inputs["mem"][bidx]),
        "ffn1_w_in": f(inputs["ffn1_w_in"][0]), "ffn1_w_out": f(inputs["ffn1_w_out"][0]),
        "ffn2_w_in": f(inputs["ffn2_w_in"][0]), "ffn2_w_out": f(inputs["ffn2_w_out"][0]),
        "w_in": f(inputs["w_in"][0]), "w_fu": f(inputs["w_fu"][0]), "b_f": f(inputs["b_f"][0]).reshape(1, 512),
        "w_pool": f(inputs["w_pool"][0]), "w_mem_kv": f(inputs["w_mem_kv"][0]),
        "w_up_gla": f(inputs["w_up_gla"][0]), "w_up_pool": f(inputs["w_up_pool"][0]),
        "w_up_xattn": f(inputs["w_up_xattn"][0]), "w_o": f(inputs["w_o"][0]),
        "gng": f(np.asarray(inputs["gla_norm_g"][0]).reshape(8, 128).T),
        "psc": f(np.asarray(inputs["pool_scale"][0]).reshape(4, 128).T),
        "c_ident": ident, "c_emat": emat, "c_ind": ind, "c_invc": invc,
    }
    for n in ("ffn1_pre_g", "ffn1_post_g", "mix_pre_g", "mix_post_g", "ffn2_pre_g", "ffn2_post_g", "final_g",
              "mem_norm_g"):
        m[n] = f(inputs[n][0])
    return m


def kernel(**inputs):
    nb = inputs["x"].shape[0]
    nc, _ = build_nc()
    in_maps = [make_in_map(inputs, b) for b in range(nb)]
    res = run_bass_kernel_spmd(nc, in_maps, core_ids=list(range(nb)))
    return np.stack([np.asarray(r["y"], dtype=np.float32) for r in res.results], axis=0)
```

```python
import numpy as np
from contextlib import ExitStack
import concourse.bass as bass
import concourse.mybir as mybir
from concourse.bass_utils import run_bass_kernel_spmd

F32 = mybir.dt.float32
BF16 = mybir.dt.bfloat16
AF = mybir.ActivationFunctionType
ALU = mybir.AluOpType

D = 1024
S_LEN = 4096
T = 512
NT = S_LEN // T
NB = T // 128
DFF = 2816
FC = DFF // 128
MEM = 256
INW = 7184
EPS = 1e-6
NSLOT = 5
SLOTW = 528
ENGS = ("pe", "act", "dve", "pool", "sp")
NDMASEM = 8
NSCR = 64


class Sched:
    def __init__(self):
        self.ops = {e: [] for e in ENGS}
        self.res = {}
        self.ndma = {e: 0 for e in ENGS}

    def op(self, eng, fn, reads=(), writes=(), dma=False):
        ops = self.ops
        idx = len(ops[eng])
        me = (eng, idx)
        deps = set()
        res = self.res
        for r in reads:
            st = res.get(r)
            if st is not None and st[0] is not None:
                deps.add(st[0])
        for w in writes:
            st = res.get(w)
            if st is not None:
                if st[0] is not None:
                    deps.add(st[0])
                deps.update(st[1])
        deps.discard(me)
        if eng == "pe":
            deps = {d for d in deps if d[0] != "pe"}
        rec = dict(fn=fn, deps=deps, dma=dma, marked=dma, dmaidx=None)
        if dma:
            rec["dmaidx"] = self.ndma[eng]
            self.ndma[eng] += 1
        ops[eng].append(rec)
        for r in reads:
            st = res.get(r)
            if st is None:
                st = res[r] = [None, []]
            if not dma:
                st[1] = [x for x in st[1] if x[0] != eng or ops[x[0]][x[1]]["dma"]]
            st[1].append(me)
        for w in writes:
            res[w] = [me, []]
        return me

    def finalize(self, sems, dmasems):
        for e in ENGS:
            for rec in self.ops[e]:
                for (de, di) in rec["deps"]:
                    self.ops[de][di]["marked"] = True
        for e in ENGS:
            cnt = 0
            for rec in self.ops[e]:
                if rec["dma"]:
                    k = rec["dmaidx"]
                    rec["tok"] = (dmasems[e][k % NDMASEM], 16 * (k // NDMASEM + 1))
                    rec["pre"] = (dmasems[e][k % NDMASEM], 16 * (k // NDMASEM))
                elif rec["marked"]:
                    cnt += 1
                    rec["tok"] = (sems[e], cnt)
        self.dmasems = dmasems

    def run_engine(self, e, h):
        waited = {}
        for rec in self.ops[e]:
            need = {}
            for (de, di) in rec["deps"]:
                s, v = self.ops[de][di]["tok"]
                if need.get(s, (None, 0))[1] < v:
                    need[s] = (s, v)
            if rec["dma"] and rec["pre"][1] > 0:
                s, v = rec["pre"]
                if need.get(s, (None, 0))[1] < v:
                    need[s] = (s, v)
            for s, (_, v) in need.items():
                if waited.get(s, 0) < v:
                    h.wait_ge(s, v)
                    waited[s] = v
            ins = rec["fn"](h)
            if rec["dma"]:
                ins.then_inc(rec["tok"][0], 16)
            elif rec["marked"]:
                ins.then_inc(rec["tok"][0], 1)
        for k in range(min(self.ndma[e], NDMASEM)):
            n = (self.ndma[e] - 1 - k) // NDMASEM + 1
            h.wait_ge(self.dmasems[e][k], 16 * n)


def build_nc(ntiles=NT, stages=("ffn1", "mix", "ffn2", "final")):
    nc = bass.Bass("TRN2", target_bir_lowering=False)

    def din(name, shape):
        return nc.dram_tensor(name, shape, F32, kind="ExternalInput").ap()

    x = din("x", [S_LEN, D])
    mem = din("mem", [MEM, D])
    w1_in = din("ffn1_w_in", [D, 2 * DFF])
    w1_out = din("ffn1_w_out", [DFF, D])
    w2_in = din("ffn2_w_in", [D, 2 * DFF])
    w2_out = din("ffn2_w_out", [DFF, D])
    w_in = din("w_in", [D, INW])
    w_fu = din("w_fu", [16, 512])
    b_f = din("b_f", [1, 512])
    w_pool = din("w_pool", [4, 128, 128])
    w_mem_kv = din("w_mem_kv", [D, D])
    w_up_gla = din("w_up_gla", [D, D])
    w_up_pool = din("w_up_pool", [512, D])
    w_up_xattn = din("w_up_xattn", [512, D])
    w_o = din("w_o", [D, D])
    gains = {n: din(n, [D]) for n in ("ffn1_pre_g", "ffn1_post_g", "mix_pre_g", "mix_post_g",
                                        "ffn2_pre_g", "ffn2_post_g", "final_g", "mem_norm_g")}
    gng_d = din("gng", [128, 8])
    psc_d = din("psc", [128, 4])
    c_ident = din("c_ident", [128, 128])
    c_emat = din("c_emat", [128, 128])
    c_ind = din("c_ind", [128, 2])
    c_invc = din("c_invc", [64])
    y = nc.dram_tensor("y", [S_LEN, D], F32, kind="ExternalOutput").ap()
    wscr = nc.dram_tensor("wscr", [NSCR, 128, 8 * SLOTW], BF16, kind="Internal").ap()

    S = Sched()
    with ExitStack() as es:
        def sb(name, shape, dt):
            return es.enter_context(nc.sbuf_tensor(name, shape, dt))

        xres = sb("xres", [128, 8, D], F32)
        gbuf = sb("gbuf", [128, 3, D], F32)
        hT = sb("hT", [128, 8, T], BF16)
        uT2 = sb("uT", [128, FC * T], BF16)
        uT = uT2[:].rearrange("p (j t) -> p j t", j=FC)
        oT = uT2[:, 0:16 * T].bitcast(F32).rearrange("p (j t) -> p j t", j=8)
        wring = sb("wring", [128, NSLOT, 8, SLOTW], BF16)
        hbuf = sb("hbuf", [128, 2, D], BF16)
        ssq = sb("ssq", [128, 8], F32)
        ssp = sb("ssp", [128, 2, NB], F32)
        rs = sb("rs", [128, 8], F32)
        satmp = sb("satmp", [128, 2, T], F32)
        tmpf = sb("tmpf", [128, 2, T], F32)
        identf = sb("identf", [128, 128], F32)
        identb = sb("identb", [128, 128], BF16)
        ones_bf = sb("ones_bf", [128, 128], BF16)
        emat = sb("emat", [128, 128], F32)
        ind = sb("ind", [128, 2], F32)
        dmy = sb("dmy", [128, 2], F32)
        gng = sb("gng_t", [128, 8], F32)
        psc = sb("psc_t", [128, 4], F32)
        invc = sb("invc", [128, 4, 16], F32)
        wpool = sb("wpool", [128, 4, 128], BF16)
        wfu_aug = sb("wfu_aug", [128, 512], F32)
        flow_aug = sb("flow_aug", [128, T], F32)
        KT = sb("KT", [128, 4, MEM], BF16)
        Vt = sb("Vt", [128, 2, 512], BF16)
        qT = sb("qT", [128, 4, T], BF16)
        kt_lo = sb("kt_lo", [128, NB, 512], BF16)
        kt_hi = sb("kt_hi", [128, NB, 512], BF16)
        v_flat = sb("v_tm", [128, NB * D], BF16)
        v_tm = v_flat[:].rearrange("p (b d) -> p b d", b=NB)
        ogT = v_flat[:].rearrange("p (j t) -> p j t", j=8)
        sgT = sb("sgT", [128, 8, T], BF16)
        xqT = sb("xqT", [128, 4, T], BF16)
        pT_ext = sb("pT_ext", [128, 4, 16 + T], F32)
        plA = sb("plA", [128, 4, 16 + T], F32)
        plB = sb("plB", [128, 4, 16 + T], F32)
        mixedT = sb("mixedT", [128, 4, T], BF16)
        la = tmpf[:, 0:1, :]
        multb = tmpf[:, 1:2, :]
        dec = sb("dec", [128, NB, 8], F32)
        state32 = sb("state32", [128, 4, 256], F32)
        state16 = sb("state16", [128, 1, 4, 256], BF16)
        PT = sb("PT", [128, 1, 2, T], BF16)
        sigt = satmp
        memnT = ogT[:, :, 0:MEM]
        macc = plA[:, :, 0:T]
        mergedT = sgT

        ps = [es.enter_context(nc.psum_tensor(f"ps{i}", [128, 512], F32)) for i in range(8)]
        sems = {e: es.enter_context(nc.semaphore("s_" + e)) for e in ENGS}
        dsems = {e: [es.enter_context(nc.semaphore(f"d_{e}{i}")) for i in range(NDMASEM)] for e in ("sp", "pool")}

        def dma(eng, out, in_, reads, writes):
            S.op(eng, lambda h: h.dma_start(out=out, in_=in_), reads, writes, dma=True)

        def act(out, in_, func, reads, writes, **kw):
            S.op("act", lambda h: h.activation(out=out, in_=in_, func=func, **kw), reads, writes)

        def stt(out, in0, scalar, in1, op0, op1, reads, writes):
            S.op("dve", lambda h: h.scalar_tensor_tensor(out=out, in0=in0, scalar=scalar, in1=in1, op0=op0, op1=op1),
                 reads, writes)

        def tt(out, in0, in1, op, reads, writes):
            S.op("dve", lambda h: h.tensor_tensor(out=out, in0=in0, in1=in1, op=op), reads, writes)

        def tsc(out, in0, s1, s2, op0, op1, reads, writes):
            if s2 is None:
                S.op("dve", lambda h: h.tensor_scalar(out=out, in0=in0, scalar1=s1, scalar2=None, op0=op0), reads, writes)
            else:
                S.op("dve", lambda h: h.tensor_scalar(out=out, in0=in0, scalar1=s1, scalar2=s2, op0=op0, op1=op1),
                     reads, writes)

        def vcopy(out, in_, reads, writes):
            S.op("dve", lambda h: h.tensor_copy(out=out, in_=in_), reads, writes)

        def vmemset(ap, val, writes):
            S.op("dve", lambda h: h.memset(ap, val), (), writes)

        def mm(out, lhsT, rhs, start, stop, reads, writes):
            S.op("pe", lambda h: h.matmul(out, lhsT=lhsT, rhs=rhs, start=start, stop=stop), reads, writes)

        def tr(out, in_, reads, writes):
            S.op("pe", lambda h: h.transpose(out=out, in_=in_, identity=identb[:]), reads, writes)

        bank_ctr = [0]
        bank_live = [False] * 8
        bank_rel = list(range(8))

        def bank():
            free = [b for b in range(8) if not bank_live[b]]
            if not free:
                raise AssertionError("no free PSUM bank")
            b = min(free, key=lambda q: bank_rel[q])
            bank_live[b] = True
            return b

        def rel(*bs):
            for b in bs:
                assert bank_live[b]
                bank_live[b] = False
                bank_ctr[0] += 1
                bank_rel[b] = 8 + bank_ctr[0]

        slab_ctr = [0]

        scr_index = {}
        scr_on = [False]

        def load_w(W, k0, nk, c0, ncols, extra_reads=()):
            s = slab_ctr[0] % NSLOT
            slab_ctr[0] += 1
            view = wring[:, s, 0:nk, 0:ncols]
            key = (id(W), k0, nk, c0, ncols)
            if key in scr_index:
                i = scr_index[key]
                src = wscr[i, :, 0:nk * ncols].rearrange("p (k n) -> p k n", k=nk)
                dma("pool", view, src, [("scr", i)], [("w", s)])
                return wring, s
            src = W[k0 * 128:(k0 + nk) * 128, c0:c0 + ncols].rearrange("(k p) n -> p k n", p=128)
            dma("pool", view, src, list(extra_reads), [("w", s)])
            if scr_on[0] and ntiles > 1:
                i = len(scr_index)
                assert i < NSCR
                scr_index[key] = i
                dst = wscr[i, :, 0:nk * ncols].rearrange("p (k n) -> p k n", k=nk)
                dma("sp", dst, view, [("w", s)], [("scr", i)])
            return wring, s

        UT16 = [("uT", j) for j in range(16)]
        PSK = lambda b: ("ps", b)
        HT_ALL = [("hT", b) for b in range(NB)]

        dma("sp", identf[:], c_ident, (), ["identf"])
        dma("sp", emat[:], c_emat, (), ["emat"])
        dma("sp", ind[:], c_ind, (), ["ind"])
        dma("sp", gng[:], gng_d, (), ["gng"])
        dma("sp", psc[:], psc_d, (), ["psc"])
        dma("sp", invc[:].rearrange("p g t -> p (g t)"), c_invc.partition_broadcast(128), (), ["invc"])
        gctr = [0]

        def load_gain(name):
            i = gctr[0] % 3
            gctr[0] += 1
            dma("sp", gbuf[:, i, :], gains[name].partition_broadcast(128), (), [("g", i)])
            return gbuf[:, i, :], ("g", i)
        vcopy(identb[:], identf[:], ["identf"], ["identb"])
        vmemset(ones_bf[:], 1.0, ["ones"])
        vmemset(dmy[:], 1.0, ["dmy0", "dmy1"])

        def table_prefetch():
            act(dmy[:, 1:2], dmy[:, 0:1], AF.Ln, ["dmy0"], ["dmy1"])
        vmemset(wfu_aug[:], 0.0, ["wfu"])
        vmemset(flow_aug[:], 0.0, ["flow"])
        vmemset(flow_aug[32:33, :], 1.0, ["flow"])
        dma("sp", wfu_aug[0:16, :], w_fu, (), ["wfu"])
        dma("sp", wfu_aug[32:33, :], b_f, (), ["wfu"])
        dma("pool", wpool[:], w_pool.rearrange("g c d -> c g d"), (), ["wpool"])
        vmemset(pT_ext[:, :, 0:16], 0.0, ["pT"])
        vmemset(state32[:], 0.0, ["state32"])
        vmemset(kt_lo[:], 0.0, [("kt", b) for b in range(NB)])
        vmemset(kt_hi[:], 0.0, [("kt", b) for b in range(NB)])

        def rstd_chain(n, scale, c0=0):
            rk = [("rs", i) for i in range(c0, c0 + n)]
            tsc(rs[:, c0:c0 + n], ssq[:, c0:c0 + n], scale, EPS, ALU.mult, ALU.add,
                [("ssq", i) for i in range(c0, c0 + n)], rk)
            act(rs[:, c0:c0 + n], rs[:, c0:c0 + n], AF.Ln, rk, rk)
            act(rs[:, c0:c0 + n], rs[:, c0:c0 + n], AF.Exp, rk, rk, scale=-0.5)

        def transpose_pe(src2d, srckey):
            bks = [bank(), bank()]
            for c in range(8):
                bk = bks[c // 4]
                mm(ps[bk][:, (c % 4) * 128:(c % 4 + 1) * 128], src2d[:, c * 128:(c + 1) * 128], identb[:], True, True,
                   [srckey, "identb"], [PSK(bk)])
            return bks

        def transpose_copy(bks, dst, b, dstkeys):
            for i2 in range(2):
                act(dst[:, 4 * i2:4 * i2 + 4, b * 128:(b + 1) * 128],
                    ps[bks[i2]][:].rearrange("p (c t) -> p c t", c=4), AF.Copy, [PSK(bks[i2])], dstkeys)
            rel(*bks)

        def transpose_block(src2d, srckey, dst, b, dstkeys):
            transpose_copy(transpose_pe(src2d, srckey), dst, b, dstkeys)

        if "mix" in stages:
            for b in range(2):
                dma("sp", xres[:, 4 + b, :], mem[b * 128:(b + 1) * 128, :], (), [("x", 4 + b)])
            for b in range(NB):
                dma("sp", xres[:, b, :], x[b * 128:(b + 1) * 128, :], (), [("x", b)])
            for b in range(2):
                act(hbuf[:, b, :], xres[:, 4 + b, :], AF.Square, [("x", 4 + b)], [("ssq", 4 + b), ("hb", b)],
                    accum_out=ssq[:, 4 + b:5 + b])
            rstd_chain(2, 1.0 / D, 4)
            gm, gmk = load_gain("mem_norm_g")
            OG_ALL = [("og", k) for k in range(8)] + [("v", b) for b in range(NB)]
            for b in range(2):
                stt(hbuf[:, b, :], xres[:, 4 + b, :], rs[:, 4 + b:5 + b], gm, ALU.mult, ALU.mult,
                    [("x", 4 + b), ("rs", 4 + b), gmk], [("hb", b)])
                transpose_block(hbuf[:, b, :], ("hb", b), memnT, b, OG_ALL)
            wr, s = load_w(w_mem_kv, 0, 8, 0, 512, extra_reads=[("x", i) for i in range(6)])
            for hh in range(4):
                bk = bank()
                for k in range(8):
                    mm(ps[bk][:, 0:MEM], wr[:, s, k, hh * 128:(hh + 1) * 128], memnT[:, k, :], k == 0, k == 7,
                       [("w", s)] + OG_ALL, [PSK(bk)])
                act(KT[:, hh, :], ps[bk][:, 0:MEM], AF.Copy, [PSK(bk)], ["KT"])
                rel(bk)
            wr, s = load_w(w_mem_kv, 0, 8, 512, 512)
            for blk in range(2):
                bk = bank()
                for k in range(8):
                    mm(ps[bk][:], memnT[:, k, blk * 128:(blk + 1) * 128], wr[:, s, k, 0:512], k == 0, k == 7,
                       [("w", s)] + OG_ALL, [PSK(bk)])
                act(Vt[:, blk, :], ps[bk][:], AF.Copy, [PSK(bk)], ["Vt"])
                rel(bk)

        def xslot(t, b):
            return 4 * (t % 2) + b

        def XK(t, b):
            return ("x", xslot(t, b))

        def prenorm_stats(t, blocks, coff):
            for b in blocks:
                act(hbuf[:, b % 2, :], xres[:, xslot(t, b), :], AF.Square, [XK(t, b)],
                    [("ssq", b + coff), ("hb", b % 2)], accum_out=ssq[:, b + coff:b + coff + 1])
            rstd_chain(len(blocks), 1.0 / D, blocks[0] + coff)

        def prenorm_scale(t, blocks, coff, gap, gkey):
            for b in blocks:
                hb = b % 2
                stt(hbuf[:, hb, :], xres[:, xslot(t, b), :], rs[:, b + coff:b + coff + 1], gap, ALU.mult, ALU.mult,
                    [XK(t, b), ("rs", b + coff), gkey], [("hb", hb)])

        def prenorm_transpose(blocks):
            prenorm_transpose_copy(prenorm_transpose_pe(blocks))

        def prenorm_transpose_pe(blocks):
            return [(b, transpose_pe(hbuf[:, b % 2, :], ("hb", b % 2))) for b in blocks]

        def prenorm_transpose_copy(pend):
            for b, bks in pend:
                transpose_copy(bks, hT, b, [("hT", b)])

        SA_FLAT = satmp[:].rearrange("p a b -> p (a b)")

        def out_and_post(t, W, src, srckey, nkc, post_gname, coef, tail, parts, resident=False, lag=1, a_lag=0):
            gap, gkey = load_gain(post_gname)
            res = None
            if resident:
                assert nkc <= 8
                res = [load_w(W, 0, nkc, n * 512, 512) for n in range(2)]
            for pi, blocks in enumerate(parts):
                banks = {}
                for n in range(2):
                    for b in blocks:
                        banks[(n, b)] = bank()
                for n in range(2):
                    k0 = 0
                    while k0 < nkc:
                        nk = min(8, nkc - k0)
                        wr, sl = res[n] if resident else load_w(W, k0, nk, n * 512, 512)
                        for kk in range(nk):
                            k = k0 + kk
                            for b in blocks:
                                bk = banks[(n, b)]
                                mm(ps[bk][:], src[:, k, b * 128:(b + 1) * 128], wr[:, sl, kk, 0:512], k == 0,
                                   k == nkc - 1, [("w", sl), (srckey, k)], [PSK(bk)])
                        k0 += nk
                tail(pi, blocks, "M")
                if pi >= lag:
                    tail(pi - lag, parts[pi - lag], "B1")
                i = 0
                for n in range(2):
                    for b in blocks:
                        bk = banks[(n, b)]
                        act(satmp[:, i % 2, :], ps[bk][:], AF.Square, [PSK(bk)], [("ssp", n, b), ("sa", i % 2)],
                            accum_out=ssp[:, n, b:b + 1])
                        i += 1
                b0, nb = blocks[0], len(blocks)
                tt(ssq[:, b0:b0 + nb], ssp[:, 0, b0:b0 + nb], ssp[:, 1, b0:b0 + nb], ALU.add,
                   [("ssp", n, b) for n in range(2) for b in blocks], [("ssq", b) for b in blocks])
                rstd_chain(nb, 1.0 / D, b0)
                tail(pi, blocks, "M2")
                if pi >= lag:
                    tail(pi - lag, parts[pi - lag], "B2")
                i = 0
                for b in blocks:
                    xs = xslot(t, b)
                    for n in range(2):
                        bk = banks[(n, b)]
                        stt(tmpf[:, i % 2, :], ps[bk][:], rs[:, b:b + 1], gap[:, n * 512:(n + 1) * 512],
                            ALU.mult, ALU.mult, [PSK(bk), ("rs", b), gkey], [("tmpf", i % 2)])
                        stt(xres[:, xs, n * 512:(n + 1) * 512], tmpf[:, i % 2, :], coef, xres[:, xs, n * 512:(n + 1) * 512],
                            ALU.mult, ALU.add, [XK(t, b), ("tmpf", i % 2)], [XK(t, b)])
                        rel(bk)
                        i += 1
                if pi >= a_lag:
                    tail(pi - a_lag, parts[pi - a_lag], "A")
            np_ = len(parts)
            done_b = max(0, np_ - lag)
            for pa in range(max(0, np_ - a_lag), np_):
                while done_b < np_ and done_b <= pa - 2:
                    tail(done_b, parts[done_b], "B1")
                    tail(done_b, parts[done_b], "B2")
                    done_b += 1
                tail(pa, parts[pa], "A")
            for pi in range(done_b, np_):
                tail(pi, parts[pi], "B1")
                tail(pi, parts[pi], "B2")

        FFN_PARTS = [(0, 1), (2, 3)]
        WO_PARTS = [(0,), (1,), (2,), (3,)]

        def ffn_hidden(Win, pre=None):
            for j0 in range(0, FC, 4):
                nj = min(4, FC - j0)
                if pre is not None and j0 == 0:
                    (wrA, sA), (wrB, sB) = pre
                else:
                    wrA, sA = load_w(Win, 0, 8, j0 * 128, nj * 128)
                    wrB, sB = load_w(Win, 0, 8, DFF + j0 * 128, nj * 128)
                for jj in range(nj):
                    j = j0 + jj
                    bA = bank()
                    for k in range(8):
                        mm(ps[bA][:], wrA[:, sA, k, jj * 128:(jj + 1) * 128], hT[:, k, :], k == 0, k == 7,
                           [("w", sA)] + HT_ALL, [PSK(bA)])
                    bB = bank()
                    for k in range(8):
                        mm(ps[bB][:], wrB[:, sB, k, jj * 128:(jj + 1) * 128], hT[:, k, :], k == 0, k == 7,
                           [("w", sB)] + HT_ALL, [PSK(bB)])
                    act(satmp[:, j % 2, :], ps[bA][:], AF.Silu, [PSK(bA)], [("sa", j % 2)])
                    tt(uT[:, j, :], satmp[:, j % 2, :], ps[bB][:], ALU.mult, [("sa", j % 2), PSK(bB)], [("uT", j)])
                    rel(bA, bB)
            table_prefetch()

        def make_v_filler():
            fs = {}

            def pe():
                fs["wrv"] = [load_w(w_in, 0, 8, 1024 + vh * 512, 512) for vh in range(2)]
                fs["banks"] = []
                for b in (0, 1):
                    for vh in range(2):
                        wr, sv = fs["wrv"][vh]
                        bkv = bank()
                        for k in range(8):
                            mm(ps[bkv][:], hT[:, k, b * 128:(b + 1) * 128], wr[:, sv, k, 0:512], k == 0, k == 7,
                               [("w", sv), ("hT", b)], [PSK(bkv)])
                        fs["banks"].append((b, vh, bkv))

            def evac():
                for b, vh, bkv in fs["banks"]:
                    act(v_tm[:, b, vh * 512:(vh + 1) * 512], ps[bkv][:], AF.Copy, [PSK(bkv)], [("v", b)])
                    rel(bkv)
            return fs, pe, evac

        def wo_phase(t):
            gap, gkey = load_gain("mix_post_g")
            gpre = load_gain("ffn2_pre_g")
            res = [load_w(w_o, 0, 8, n * 512, 512) for n in range(2)]
            pre = (load_w(w2_in, 0, 8, 0, 512), load_w(w2_in, 0, 8, DFF, 512))
            banks = {}

            def MM(b):
                for n in range(2):
                    banks[(n, b)] = bank()
                for n in range(2):
                    wr, sl = res[n]
                    bk = banks[(n, b)]
                    for k in range(8):
                        mm(ps[bk][:], mergedT[:, k, b * 128:(b + 1) * 128], wr[:, sl, k, 0:512], k == 0, k == 7,
                           [("w", sl), ("sg", k)], [PSK(bk)])

            def S1(b):
                for n in range(2):
                    bk = banks[(n, b)]
                    act(satmp[:, n, :], ps[bk][:], AF.Square, [PSK(bk)], [("ssp", n, b), ("sa", n)],
                        accum_out=ssp[:, n, b:b + 1])
                tt(ssq[:, b:b + 1], ssp[:, 0, b:b + 1], ssp[:, 1, b:b + 1], ALU.add,
                   [("ssp", 0, b), ("ssp", 1, b)], [("ssq", b)])
                rstd_chain(1, 1.0 / D, b)

            def S2(b):
                xs = xslot(t, b)
                for n in range(2):
                    bk = banks[(n, b)]
                    stt(tmpf[:, n, :], ps[bk][:], rs[:, b:b + 1], gap[:, n * 512:(n + 1) * 512],
                        ALU.mult, ALU.mult, [PSK(bk), ("rs", b), gkey], [("tmpf", n)])
                    S.op("pool", lambda h, o=xres[:, xs, n * 512:(n + 1) * 512], i1=tmpf[:, n, :]:
                         h.tensor_tensor(out=o, in0=o, in1=i1, op=ALU.add), [XK(t, b), ("tmpf", n)], [XK(t, b)])
                    rel(bk)

            def S3(b):
                act(SA_FLAT, xres[:, xslot(t, b), :], AF.Square, [XK(t, b)], [("ssq", b), ("sa", 0), ("sa", 1)],
                    accum_out=ssq[:, b:b + 1])
                rstd_chain(1, 1.0 / D, b)

            def S4(b):
                prenorm_scale(t, (b,), 0, *gpre)

            def B(b):
                prenorm_transpose((b,))

            MM(0); S1(0)
            MM(1); S1(1); S2(0)
            MM(2); S1(2); S2(1); S3(0)
            MM(3); S1(3); S2(2); S3(1); S4(0)
            S2(3); S3(2); S4(1); B(0)
            S3(3); S4(2); B(1)
            S4(3); B(2); B(3)
            return pre

        def make_prenorm_tail(t, gname, filler=None, nparts=2):
            st = {}

            def tail(pi, blocks, phase):
                if phase in ("M", "M2"):
                    return
                if phase == "B1":
                    st[("p", pi)] = prenorm_transpose_pe(blocks)
                    return
                if phase == "B2":
                    prenorm_transpose_copy(st.pop(("p", pi)))
                    if filler is not None and pi == 0:
                        filler[1]()
                    return
                if pi == 0:
                    st["g"] = load_gain(gname)
                prenorm_stats(t, blocks, 0)
                prenorm_scale(t, blocks, 0, *st["g"])
                if filler is not None and pi == nparts - 1:
                    filler[2]()
            return tail

        def mixer(t, vfill=None):
            sc = 128.0 ** -0.5
            wr, s = load_w(w_in, 0, 8, 3072, 528)
            bk = bank()
            for k in range(8):
                mm(ps[bk][:], wr[:, s, k, 0:128], hT[:, k, :], k == 0, k == 7, [("w", s)] + HT_ALL, [PSK(bk)])
            vcopy(flow_aug[0:16, :], ps[bk][0:16, :], [PSK(bk)], ["flow"])
            rel(bk)
            for g in range(4):
                bk = bank()
                for k in range(8):
                    mm(ps[bk][:], wr[:, s, k, 16 + g * 128:16 + (g + 1) * 128], hT[:, k, :], k == 0, k == 7,
                       [("w", s)] + HT_ALL, [PSK(bk)])
                act(pT_ext[:, g, 16:16 + T], ps[bk][:], AF.Copy, [PSK(bk)], [("pT", g)])
                rel(bk)
            wrk, sk = load_w(w_in, 0, 8, 512, 512)
            wrv = vfill["wrv"]
            qslab = load_w(w_in, 0, 8, 0, 512)
            xqslab = [None]

            def qproj(which, chunks):
                wr, s_ = qslab if which == "q" else xqslab[0]
                dst, key = (qT, "qT") if which == "q" else (xqT, "xq")
                for hh in chunks:
                    bkq = bank()
                    for k in range(8):
                        mm(ps[bkq][:], wr[:, s_, k, hh * 128:(hh + 1) * 128], hT[:, k, :], k == 0, k == 7,
                           [("w", s_)] + HT_ALL, [PSK(bkq)])
                    act(dst[:, hh, :], ps[bkq][:], AF.Identity, [PSK(bkq)], [(key, hh)], scale=sc)
                    rel(bkq)

            for b in (2, 3, 0, 1):
                pb = 0
                if b == 1:
                    xqslab[0] = load_w(w_in, 0, 8, 3600, 512)
                bk = bank()
                mm(ps[bk][:], flow_aug[:, b * 128:(b + 1) * 128], wfu_aug[:], True, True, ["flow", "wfu"], [PSK(bk)])
                act(la[:, pb, :], ps[bk][:], AF.Exp, [PSK(bk)], [("tmpf", 0)], scale=-1.0)
                rel(bk)
                act(la[:, pb, :], la[:, pb, :], AF.Ln, [("tmpf", 0)], [("tmpf", 0)], bias=1.0)
                if b < 2:
                    qproj("q" if b == 0 else "xq", (0, 1))
                for vh in range(2):
                    if b < 2:
                        continue
                    wr, sv = wrv[vh]
                    bkv = bank()
                    for k in range(8):
                        mm(ps[bkv][:], hT[:, k, b * 128:(b + 1) * 128], wr[:, sv, k, 0:512], k == 0, k == 7,
                           [("w", sv), ("hT", b)], [PSK(bkv)])
                    act(v_tm[:, b, vh * 512:(vh + 1) * 512], ps[bkv][:], AF.Copy, [PSK(bkv)], [("v", b)])
                    rel(bkv)
                bk = bank()
                mm(ps[bk][:], emat[:], la[:, pb, :], True, True, ["emat", ("tmpf", 0)], [PSK(bk)])
                act(multb[:, pb, :], ps[bk][:], AF.Exp, [PSK(bk)], [("tmpf", 1)])
                rel(bk)
                bk2 = bank()
                for hh in range(4):
                    mm(ps[bk2][:, hh * 2:(hh + 1) * 2], la[:, pb, hh * 128:(hh + 1) * 128], ind[:], True, True,
                       ["ind", ("tmpf", 0)], [PSK(bk2)])
                act(dec[:, b, :], ps[bk2][:, 0:8], AF.Exp, [PSK(bk2)], [("dec", b)])
                rel(bk2)
                if b < 2:
                    qproj("q" if b == 0 else "xq", (2, 3))
                bk3 = bank()
                for k in range(8):
                    mm(ps[bk3][:], hT[:, k, b * 128:(b + 1) * 128], wrk[:, sk, k, 0:512], k == 0, k == 7,
                       [("w", sk), ("hT", b)], [PSK(bk3)])
                tt(kt_lo[0:64, b, :], ps[bk3][0:64, :], multb[0:64, pb, :], ALU.mult, [PSK(bk3), ("tmpf", 1)],
                   [("kt", b)])
                tt(kt_hi[64:128, b, :], ps[bk3][64:128, :], multb[64:128, pb, :], ALU.mult, [PSK(bk3), ("tmpf", 1)],
                   [("kt", b)])
                rel(bk3)
            for gh in range(2):
                wr, s = load_w(w_in, 0, 8, 2048 + gh * 512, 512)
                for jj in range(4):
                    bk = bank()
                    for k in range(8):
                        mm(ps[bk][:], wr[:, s, k, jj * 128:(jj + 1) * 128], hT[:, k, :], k == 0, k == 7,
                           [("w", s)] + HT_ALL, [PSK(bk)])
                    act(sgT[:, gh * 4 + jj, :], ps[bk][:], AF.Silu, [PSK(bk)], [("sg", gh * 4 + jj)])
                    rel(bk)
            table_prefetch()
            PT_ALL = [("pT", g) for g in range(4)]
            L = 16 + T
            tt(plA[:, :, 1:L], pT_ext[:, :, 1:L], pT_ext[:, :, 0:L - 1], ALU.add, PT_ALL, ["plA"])
            tt(plB[:, 1:4, 3:L], plA[:, 1:4, 3:L], plA[:, 1:4, 1:L - 2], ALU.add, ["plA"], ["plB"])
            tt(plA[:, 2:4, 7:L], plB[:, 2:4, 7:L], plB[:, 2:4, 3:L - 4], ALU.add, ["plB", "plA"], ["plA"])
            tt(plB[:, 3, 15:L], plA[:, 3, 15:L], plA[:, 3, 7:L - 8], ALU.add, ["plA", "plB"], ["plB"])
            for g in range(4):
                src = plA if g in (0, 2) else plB
                w = float(2 ** (g + 1))
                stt(mixedT[:, g, :], src[:, g, 16:L], 1.0 / w, pT_ext[:, g, 16:L], ALU.mult, ALU.subtract,
                    ["plA", "plB", ("pT", g)], [("mixed", g)])
                if t == 0:
                    tt(tmpf[:, 0, 0:16], src[:, g, 16:32], invc[:, g, :], ALU.mult, ["plA", "plB", "invc"], [("tmpf", 0)])
                    tt(mixedT[:, g, 0:16], tmpf[:, 0, 0:16], pT_ext[:, g, 16:32], ALU.subtract,
                       [("tmpf", 0), ("pT", g)], [("mixed", g)])
            for g in range(4):
                bk = bank()
                mm(ps[bk][:], wpool[:, g, :], mixedT[:, g, :], True, True, ["wpool", ("mixed", g)], [PSK(bk)])
                act(mixedT[:, g, :], ps[bk][:], AF.Identity, [PSK(bk)], [("mixed", g)], scale=psc[:, g:g + 1])
                rel(bk)
            vcopy(pT_ext[:, :, 0:16], pT_ext[:, :, T:T + 16], PT_ALL, PT_ALL)

            def xattn_scores(hh):
                pp = 0
                for mc in range(2):
                    bk = bank()
                    mm(ps[bk][:], KT[:, hh, mc * 128:(mc + 1) * 128], xqT[:, hh, :], True, True, ["KT", ("xq", hh)],
                       [PSK(bk)])
                    act(PT[:, pp, mc, :], ps[bk][:], AF.Exp, [PSK(bk)], [("PT", pp, mc)])
                    rel(bk)

            def xattn_pv(hh):
                pp = 0
                tp_ = hh % 2
                bd = bank()
                for mc in range(2):
                    mm(ps[bd][:], ones_bf[:], PT[:, pp, mc, :], mc == 0, mc == 1, ["ones", ("PT", pp, mc)], [PSK(bd)])
                bv = bank()
                for mc in range(2):
                    mm(ps[bv][:], Vt[:, mc, hh * 128:(hh + 1) * 128], PT[:, pp, mc, :], mc == 0, mc == 1,
                       ["Vt", ("PT", pp, mc)], [PSK(bv)])
                act(ps[bd][:], ps[bd][:], AF.Ln, [PSK(bd)], [PSK(bd)])
                act(tmpf[:, tp_, :], ps[bd][:], AF.Exp, [PSK(bd)], [("tmpf", tp_)], scale=-1.0)
                tt(xqT[:, hh, :], ps[bv][:], tmpf[:, tp_, :], ALU.mult, [PSK(bv), ("tmpf", tp_)], [("xq", hh)])
                rel(bd, bv)

            for hh in range(4):
                xattn_scores(hh)
                xattn_pv(hh)
            act(dmy[:, 1:2], dmy[:, 0:1], AF.Sigmoid, ["dmy0"], ["dmy1"])

            ups = [(w_up_gla, ogT, "og", 8), (w_up_pool, mixedT, "mixed", 4), (w_up_xattn, xqT, "xq", 4)]

            def merge_slabs(i, jg):
                Wu, src, skey, nk = ups[i]
                wrg, sg_ = load_w(w_in, 0, 8, 4112 + i * 1024 + jg * 512, 512)
                wru, su = load_w(Wu, 0, nk, jg * 512, 512)
                return (wrg, sg_, wru, su)

            def merge_unit(step, i, jg, jj, slabs):
                merge_evac(*merge_mm(step, i, jg, jj, slabs))

            def merge_mm(step, i, jg, jj, slabs):
                Wu, src, skey, nk = ups[i]
                wrg, sg_, wru, su = slabs
                j = jg * 4 + jj
                mbuf = plA if jg == 0 else plB
                mk = "plA" if jg == 0 else "plB"
                macc_j = mbuf[:, jj, 0:T]
                bz = bank()
                for k in range(8):
                    mm(ps[bz][:], wrg[:, sg_, k, jj * 128:(jj + 1) * 128], hT[:, k, :], k == 0, k == 7,
                       [("w", sg_)] + HT_ALL, [PSK(bz)])
                by = bank()
                for k in range(nk):
                    mm(ps[by][:], wru[:, su, k, jj * 128:(jj + 1) * 128], src[:, k, :], k == 0, k == nk - 1,
                       [("w", su), (skey, k)] + ([("v", k // 2)] if i == 0 else []), [PSK(by)])
                return (step, i, jj, j, macc_j, mk, bz, by)

            def merge_evac(step, i, jj, j, macc_j, mk, bz, by):
                sp_ = (i * 4 + jj) % 2
                act(sigt[:, sp_, :], ps[bz][:], AF.Sigmoid, [PSK(bz)], [("sa", sp_)])
                rel(bz)
                if step == 0:
                    tt(macc_j, sigt[:, sp_, :], ps[by][:], ALU.mult, [("sa", sp_), PSK(by), mk], [("macc", j), mk])
                else:
                    tt(tmpf[:, sp_, :], sigt[:, sp_, :], ps[by][:], ALU.mult, [("sa", sp_), PSK(by)], [("tmpf", sp_)])
                    if step == 1:
                        tt(macc_j, macc_j, tmpf[:, sp_, :], ALU.add, [("macc", j), ("tmpf", sp_), mk], [("macc", j), mk])
                    else:
                        tt(mergedT[:, j, :], macc_j, tmpf[:, sp_, :], ALU.add, [("macc", j), ("tmpf", sp_), mk],
                           [("sg", j)])
                rel(by)

            def gla_U(c):
                b = c // 2
                ktx = kt_lo if c % 2 == 0 else kt_hi
                bus = [bank(), bank()]
                for hh in range(4):
                    bu = bus[hh // 2]
                    col = (hh % 2) * 256
                    mm(ps[bu][:, col:col + 256], ktx[:, b, hh * 128:(hh + 1) * 128], v_tm[:, b, hh * 256:(hh + 1) * 256],
                       True, True, [("kt", b), ("v", b)], [PSK(bu)])
                return bus

            def gla_rest(c, bus, hook=None):
                b = c // 2
                half = c % 2
                for hh in range(4):
                    bu = bus[hh // 2]
                    col = (hh % 2) * 256
                    stt(state32[:, hh, :], state32[:, hh, :], dec[:, b, hh * 2 + half:hh * 2 + half + 1],
                        ps[bu][:, col:col + 256], ALU.mult, ALU.add, [("st32", hh), ("dec", b), PSK(bu)], [("st32", hh)])
                    act(state16[:, 0, hh, :], state32[:, hh, :], AF.Copy, [("st32", hh)], [("st16", hh)])
                rel(*bus)
                if hook is not None:
                    hook()
                bo = bank()
                for hh in range(4):
                    for e in range(2):
                        mm(ps[bo][:, (hh * 2 + e) * 64:(hh * 2 + e + 1) * 64], state16[:, 0, hh, e * 128:(e + 1) * 128],
                           qT[:, hh, c * 64:(c + 1) * 64], True, True, [("st16", hh), ("qT", hh)], [PSK(bo)])
                act(oT[:, :, c * 64:(c + 1) * 64], ps[bo][:].rearrange("p (j t) -> p j t", j=8), AF.Copy, [PSK(bo)],
                    [("oT", c)] + UT16)
                rel(bo)

            bus_next = gla_U(0)
            pending = [None]
            for c in range(8):
                bus_cur = bus_next
                if c < 7:
                    bus_next = gla_U(c + 1)
                pend = pending[0]
                gla_rest(c, bus_cur, (lambda p=pend: merge_evac(*p)) if pend is not None else None)
                if c % 4 == 0:
                    pslabs = merge_slabs(1, c // 4)
                pending[0] = merge_mm(0, 1, c // 4, c % 4, pslabs)
            merge_evac(*pending[0])
            OT_ALL = [("oT", c) for c in range(8)] + UT16
            xslabs0 = merge_slabs(2, 0)
            pend0 = merge_mm(1, 2, 0, 0, xslabs0)

            def sqbuf(hh, e):
                if hh < 2:
                    return qT[:, hh * 2 + e, :], ("qT", hh * 2 + e)
                return mixedT[:, (hh - 2) * 2 + e, :], ("mixed", (hh - 2) * 2 + e)

            for hh in range(4):
                for e in range(2):
                    sq_ap, sq_key = sqbuf(hh, e)
                    act(sq_ap, oT[:, hh * 2 + e, :], AF.Square, OT_ALL, [sq_key])
            obanks = []
            for hh in range(4):
                bk = bank()
                obanks.append(bk)
                for e in range(2):
                    sq_ap, sq_key = sqbuf(hh, e)
                    mm(ps[bk][:], ones_bf[:], sq_ap, e == 0, e == 1, ["ones", sq_key], [PSK(bk)])
            for hh in range(4):
                bk = obanks[hh]
                act(ps[bk][:], ps[bk][:], AF.Ln, [PSK(bk)], [PSK(bk)], scale=1.0 / 256, bias=EPS)
                act(ps[bk][:], ps[bk][:], AF.Exp, [PSK(bk)], [PSK(bk)], scale=-0.5)
            for hh in range(4):
                bk = obanks[hh]
                for e in range(2):
                    he = hh * 2 + e
                    stt(tmpf[:, e, :], oT[:, he, :], gng[:, he:he + 1], ps[bk][:], ALU.mult, ALU.mult,
                        OT_ALL + ["gng", PSK(bk)], [("tmpf", e)])
                    tt(ogT[:, he, :], tmpf[:, e, :], sgT[:, he, :], ALU.mult, [("tmpf", e), ("sg", he)],
                       [("og", he), ("v", he // 2)])
                rel(bk)
            merge_evac(*pend0)
            for jg in range(2):
                for step, i in ((1, 2), (2, 0)):
                    slabs = xslabs0 if (jg, i) == (0, 2) else merge_slabs(i, jg)
                    for jj in range(4):
                        if (jg, i, jj) == (0, 2, 0):
                            continue
                        merge_unit(step, i, jg, jj, slabs)
            table_prefetch()
            return wo_phase(t)

        def load_x(t, blocks):
            for b in blocks:
                r0 = (t * NB + b) * 128
                dma("sp", xres[:, xslot(t, b), :], x[r0:r0 + 128, :], (), [XK(t, b)])

        def make_final_tail(t):
            st = {}
            last = (t + 1 >= ntiles)

            def final_blocks(blocks):
                gfa, gfk = st["gf"]
                if "final" in stages:
                    for b in blocks:
                        act(SA_FLAT, xres[:, xslot(t, b), :], AF.Square, [XK(t, b)],
                            [("ssq", b), ("sa", 0), ("sa", 1)], accum_out=ssq[:, b:b + 1])
                    rstd_chain(len(blocks), 1.0 / D, blocks[0])
                for b in blocks:
                    xs = xslot(t, b)
                    r0 = (t * NB + b) * 128
                    if "final" in stages:
                        stt(xres[:, xs, :], xres[:, xs, :], rs[:, b:b + 1], gfa, ALU.mult, ALU.mult,
                            [XK(t, b), ("rs", b), gfk], [XK(t, b)])
                    dma("sp", y[r0:r0 + 128, :], xres[:, xs, :], [XK(t, b)], ())

            def pre():
                st["gf"] = load_gain("final_g")
                if not last:
                    st["gn"] = load_gain("ffn1_pre_g")
                    prenorm_stats(t + 1, (0, 1), 4)
                    prenorm_scale(t + 1, (0, 1), 4, *st["gn"])

            def tail(pi, blocks, phase):
                if last:
                    if phase == "A":
                        final_blocks(blocks)
                    return
                if phase == "M":
                    prenorm_transpose((0, 1) if pi == 0 else (2, 3))
                elif phase == "A":
                    if pi == 0:
                        prenorm_stats(t + 1, (2, 3), 4)
                        prenorm_scale(t + 1, (2, 3), 4, *st["gn"])
                    final_blocks(blocks)
            return tail, pre

        scr_on[0] = True
        if "mix" not in stages:
            load_x(0, (0, 1, 2, 3))
        g0 = load_gain("ffn1_pre_g")
        for hf in range(2):
            prenorm_stats(0, (2 * hf, 2 * hf + 1), 0)
            prenorm_scale(0, (2 * hf, 2 * hf + 1), 0, *g0)
            prenorm_transpose((2 * hf, 2 * hf + 1))
        for t in range(ntiles):
            ffn_hidden(w1_in)
            vf = make_v_filler()
            out_and_post(t, w1_out, uT, "uT", FC, "ffn1_post_g", 0.5,
                         make_prenorm_tail(t, "mix_pre_g", filler=vf, nparts=len(FFN_PARTS)), FFN_PARTS)
            if t + 1 < ntiles:
                load_x(t + 1, (0, 1, 2, 3))
            pre2 = mixer(t, vfill=vf[0])
            ffn_hidden(w2_in, pre=pre2)
            ftail, fpre = make_final_tail(t)
            fpre()
            out_and_post(t, w2_out, uT, "uT", FC, "ffn2_post_g", 0.5, ftail, FFN_PARTS)

        S.finalize(sems, dsems)
        with nc.Block() as block:
            @block.sync
            def _(h):
                S.run_engine("sp", h)

            @block.gpsimd
            def _(h):
                S.run_engine("pool", h)

            @block.scalar
            def _(h):
                S.run_engine("act", h)

            @block.vector
            def _(h):
                S.run_engine("dve", h)

            @block.tensor
            def _(h):
                S.run_engine("pe", h)
    return nc, S


def _consts():
    ident = np.eye(128, dtype=np.float32)
    sidx = np.arange(128)
    same = (sidx[:, None] // 64) == (sidx[None, :] // 64)
    emat = np.where(same & (sidx[:, None] > sidx[None, :]), -1.0 / 16.0, 0.0).astype(np.float32)
    ind = np.where((sidx[:, None] // 64) == np.arange(2)[None, :], -1.0 / 16.0, 0.0).astype(np.float32)
    invc = np.zeros((4, 16), np.float32)
    for g in range(4):
        w = 2 ** (g + 1)
        invc[g] = 1.0 / np.minimum(np.arange(1, 17), w)
    return ident, emat, ind, invc.reshape(64)


def make_in_map(inputs, bidx):
    f = lambda a: np.ascontiguousarray(np.asarray(a, dtype=np.float32))
    ident, emat, ind, invc = _consts()
    m = {
        "x": f(inputs["x"][bidx]), "mem": f(inputs["mem"][bidx]),
        "ffn1_w_in": f(inputs["ffn1_w_in"][0]), "ffn1_w_out": f(inputs["ffn1_w_out"][0]),
        "ffn2_w_in": f(inputs["ffn2_w_in"][0]), "ffn2_w_out": f(inputs["ffn2_w_out"][0]),
        "w_in": f(inputs["w_in"][0]), "w_fu": f(inputs["w_fu"][0]), "b_f": f(inputs["b_f"][0]).reshape(1, 512),
        "w_pool": f(inputs["w_pool"][0]), "w_mem_kv": f(inputs["w_mem_kv"][0]),
        "w_up_gla": f(inputs["w_up_gla"][0]), "w_up_pool": f(inputs["w_up_pool"][0]),
        "w_up_xattn": f(inputs["w_up_xattn"][0]), "w_o": f(inputs["w_o"][0]),
        "gng": f(np.asarray(inputs["gla_norm_g"][0]).reshape(8, 128).T),
        "psc": f(np.asarray(inputs["pool_scale"][0]).reshape(4, 128).T),
        "c_ident": ident, "c_emat": emat, "c_ind": ind, "c_invc": invc,
    }
    for n in ("ffn1_pre_g", "ffn1_post_g", "mix_pre_g", "mix_post_g", "ffn2_pre_g", "ffn2_post_g", "final_g",
              "mem_norm_g"):
        m[n] = f(inputs[n][0])
    return m


def kernel(**inputs):
    nb = inputs["x"].shape[0]
    nc, _ = build_nc()
    in_maps = [make_in_map(inputs, b) for b in range(nb)]
    res = run_bass_kernel_spmd(nc, in_maps, core_ids=list(range(nb)))
    return np.stack([np.asarray(r["y"], dtype=np.float32) for r in res.results], axis=0)
```
